# Optimizing a Trainium2 kernel written in Bass

```python
import jax, jax.numpy as jnp
from jax import lax
import numpy as np

D_MODEL = 1024
BATCH = 8
SEQ = 4096
DEPTH = 1

D_LRU = D_MODEL
LRU_HEADS = 16
LRU_HEAD_DIM = D_LRU // LRU_HEADS
LRU_CONV = 4
LRU_C = 8.0
D_CONF = D_MODEL
CONF_KERNEL = 31
D_FF = 3 * D_MODEL
FFN_CONV = 3
EPS = 1e-6
IN_SIZES = (D_LRU, D_LRU, D_CONF, D_CONF, D_MODEL, D_MODEL)
IN_SPLITS = tuple(int(s) for s in np.cumsum(IN_SIZES)[:-1])
D_IN = int(sum(IN_SIZES))

kernel_name = "hybrid_rglru_conformer_convffn"


def rmsnorm(x, g):
    x32 = x.astype(jnp.float32)
    y = x32 * lax.rsqrt(jnp.mean(x32 * x32, axis=-1, keepdims=True) + EPS)
    return (y * g.astype(jnp.float32)).astype(x.dtype)


def layernorm(x, g, b):
    x32 = x.astype(jnp.float32)
    mu = jnp.mean(x32, axis=-1, keepdims=True)
    xc = x32 - mu
    var = jnp.mean(xc * xc, axis=-1, keepdims=True)
    y = xc * lax.rsqrt(var + EPS)
    return (y * g.astype(jnp.float32) + b.astype(jnp.float32)).astype(x.dtype)


def causal_dwconv(x, w, b):
    k = w.shape[0]
    y = lax.conv_general_dilated(
        x, w[:, None, :].astype(x.dtype), window_strides=(1,), padding=[(k - 1, 0)],
        dimension_numbers=("NWC", "WIO", "NWC"), feature_group_count=x.shape[-1])
    return y + b.astype(x.dtype)


def _linear_rec_combine(left, right):
    a_l, b_l = left
    a_r, b_r = right
    return a_l * a_r, a_r * b_l + b_r


def rg_lru(xc, wa, ba, wx, bx, lam):
    bsz, s, _ = xc.shape
    xh = xc.reshape(bsz, s, LRU_HEADS, LRU_HEAD_DIM)
    r = jax.nn.sigmoid(jnp.einsum("bshd,hde->bshe", xh, wa).reshape(bsz, s, D_LRU) + ba)
    i = jax.nn.sigmoid(jnp.einsum("bshd,hde->bshe", xh, wx).reshape(bsz, s, D_LRU) + bx)
    log_a = -LRU_C * r.astype(jnp.float32) * jax.nn.softplus(-lam.astype(jnp.float32))
    a = jnp.exp(log_a)
    mult = jnp.sqrt(jnp.maximum(-jnp.expm1(2.0 * log_a), 0.0))
    u = mult * (i * xc).astype(jnp.float32)
    _, h = lax.associative_scan(_linear_rec_combine, (a, u), axis=1)
    return h.astype(xc.dtype)


def setup_inputs(seed: int = 0) -> dict:
    key = jax.random.key(seed)
    ks = jax.random.split(key, 32)
    f32 = jnp.float32

    def nrm(k, shape, fan_in):
        return jax.random.normal(k, shape, f32) * (fan_in ** -0.5)

    def gain(k, shape):
        return 1.0 + 0.02 * jax.random.normal(k, shape, f32)

    def small(k, shape):
        return 0.02 * jax.random.normal(k, shape, f32)

    L = DEPTH
    a0 = jax.random.uniform(ks[9], (L, D_LRU), f32, minval=0.9, maxval=0.999)
    base = a0 ** (1.0 / LRU_C)
    lam = jnp.log(base) - jnp.log1p(-base)
    return {
        "x": jax.random.normal(ks[0], (BATCH, SEQ, D_MODEL), f32),
        "g_mix": gain(ks[1], (L, D_MODEL)),
        "w_in": nrm(ks[2], (L, D_MODEL, D_IN), D_MODEL),
        "lru_conv_w": nrm(ks[3], (L, LRU_CONV, D_LRU), LRU_CONV),
        "lru_conv_b": small(ks[4], (L, D_LRU)),
        "lru_wa": nrm(ks[5], (L, LRU_HEADS, LRU_HEAD_DIM, LRU_HEAD_DIM), LRU_HEAD_DIM),
        "lru_ba": small(ks[6], (L, D_LRU)),
        "lru_wx": nrm(ks[7], (L, LRU_HEADS, LRU_HEAD_DIM, LRU_HEAD_DIM), LRU_HEAD_DIM),
        "lru_bx": small(ks[8], (L, D_LRU)),
        "lru_lambda": lam,
        "w_lru_out": nrm(ks[10], (L, D_LRU, D_MODEL), D_LRU),
        "conf_dw_w": nrm(ks[11], (L, CONF_KERNEL, D_CONF), CONF_KERNEL),
        "conf_dw_b": small(ks[12], (L, D_CONF)),
        "conf_ln_g": gain(ks[13], (L, D_CONF)),
        "conf_ln_b": small(ks[14], (L, D_CONF)),
        "w_conf_out": nrm(ks[15], (L, D_CONF, D_MODEL), D_CONF),
        "b_gate": small(ks[16], (L, 2 * D_MODEL)),
        "w_out": nrm(ks[17], (L, D_MODEL, D_MODEL), D_MODEL),
        "g_ffn": gain(ks[18], (L, D_MODEL)),
        "w_up": nrm(ks[19], (L, D_MODEL, 2 * D_FF), D_MODEL),
        "ffn_dw_w": nrm(ks[20], (L, FFN_CONV, D_FF), FFN_CONV),
        "ffn_dw_b": small(ks[21], (L, D_FF)),
        "w_down": nrm(ks[22], (L, D_FF, D_MODEL), D_FF),
        "g_final": gain(ks[23], (D_MODEL,)),
    }


def reference(x, g_mix, w_in, lru_conv_w, lru_conv_b, lru_wa, lru_ba, lru_wx, lru_bx,
              lru_lambda, w_lru_out, conf_dw_w, conf_dw_b, conf_ln_g, conf_ln_b, w_conf_out,
              b_gate, w_out, g_ffn, w_up, ffn_dw_w, ffn_dw_b, w_down, g_final):
    for l in range(DEPTH):
        h = rmsnorm(x, g_mix[l])
        u = jnp.einsum("bsd,de->bse", h, w_in[l])
        xa, ga, ca, cb, sa, sb = jnp.split(u, IN_SPLITS, axis=-1)

        xa = causal_dwconv(xa, lru_conv_w[l], lru_conv_b[l])
        ha = rg_lru(xa, lru_wa[l], lru_ba[l], lru_wx[l], lru_bx[l], lru_lambda[l])
        y_a = jnp.einsum("bsc,cd->bsd", ha * jax.nn.gelu(ga, approximate=True), w_lru_out[l])

        cg = ca * jax.nn.sigmoid(cb)
        cg = causal_dwconv(cg, conf_dw_w[l], conf_dw_b[l])
        cg = jax.nn.silu(layernorm(cg, conf_ln_g[l], conf_ln_b[l]))
        y_b = jnp.einsum("bsc,cd->bsd", cg, w_conf_out[l])

        gates = jax.nn.sigmoid(jnp.concatenate([sa, sb], axis=-1) + b_gate[l])
        gate_a, gate_b = jnp.split(gates, 2, axis=-1)
        merged = gate_a * y_a + gate_b * y_b
        x = x + jnp.einsum("bsd,de->bse", merged, w_out[l])

        h = rmsnorm(x, g_ffn[l])
        up = jnp.einsum("bsd,df->bsf", h, w_up[l])
        g, v = jnp.split(up, 2, axis=-1)
        g = causal_dwconv(g, ffn_dw_w[l], ffn_dw_b[l])
        x = x + jnp.einsum("bsf,fd->bsd", jax.nn.gelu(g, approximate=True) * v, w_down[l])

    return rmsnorm(x, g_final)
```

```python
import numpy as np
from contextlib import ExitStack
import concourse.bass as bass
import concourse.mybir as mybir
from concourse.bass_utils import run_bass_kernel_spmd

F32 = mybir.dt.float32
BF16 = mybir.dt.bfloat16
AF = mybir.ActivationFunctionType
ALU = mybir.AluOpType

P = 128
S = 4096
D = 1024
T = 512
NCH = S // T
KD = 8
DFF = 3072
KF = 24
EPS = 1e-6
K_GELU = 0.7978845608028654
C_GELU = 0.21145921
NGRP = 36
NSLOT = 4
N_TMP = 20
N_TB = 4

_CV = {}
def _mk_cv():
    o = 0
    for name, n in (("g_mix", 8), ("lcw", 32), ("lcb", 8), ("lba", 8), ("lbx", 8), ("lam", 8),
                    ("cw", 248), ("cb", 8), ("lng", 8), ("lnb", 8), ("bga", 8), ("bgb", 8),
                    ("g_ffn", 8), ("fw", 72), ("fb", 24), ("g_fin", 8)):
        _CV[name] = o
        o += n
    return o
NV = _mk_cv()
_DV = {n: 8 * i for i, n in enumerate(("hba", "hbx", "hbga", "hbgb", "c1", "hc1", "hlng", "hlnb", "e", "l"))}
NDV = 80


class Buf:
    __slots__ = ("name", "w", "r")

    def __init__(self, name):
        self.name = name
        self.w = None
        self.r = {}


class Prog:
    ENGS = ("pe", "act", "dve", "pool", "sp")

    def __init__(self):
        self.ops = {e: [] for e in self.ENGS}
        self.cnt = {e: 0 for e in self.ENGS}
        self.waited = {e: {} for e in self.ENGS}
        self.dcnt = {}

    def _deps(self, eng, reads, writes):
        need = {}

        def add(k, v):
            if k == "pe" and eng == "pe":
                return
            if need.get(k, 0) < v:
                need[k] = v
        for b in reads:
            if b.w is not None:
                add(*b.w)
        for b in writes:
            if b.w is not None:
                add(*b.w)
            for k, v in b.r.items():
                add(k, v)
        wd = self.waited[eng]
        waits = []
        for k, v in need.items():
            if wd.get(k, 0) < v:
                wd[k] = v
                waits.append((k, v))
        return waits

    def _mark(self, tok, reads, writes):
        k, v = tok
        for b in reads:
            if b.r.get(k, 0) < v:
                b.r[k] = v
        for b in writes:
            b.w = tok
            b.r = {}

    def op(self, eng, emit, reads=(), writes=(), inc=True):
        waits = self._deps(eng, reads, writes)
        if inc:
            self.cnt[eng] += 1
            tok = (eng, self.cnt[eng])
        else:
            tok = (eng, self.cnt[eng] + 1)
        self._mark(tok, reads, writes)
        self.ops[eng].append((waits, emit, inc, None))

    def dma(self, eng, semkey, out, in_, reads=(), writes=()):
        waits = self._deps(eng, reads, writes)
        self.dcnt[semkey] = self.dcnt.get(semkey, 0) + 16
        tok = (semkey, self.dcnt[semkey])
        self._mark(tok, reads, writes)
        self.ops[eng].append((waits, (lambda e, o=out, i=in_: e.dma_start(out=o, in_=i)), False, semkey))

    def wait_all(self, eng, keys):
        waits = []
        for k in keys:
            v = self.dcnt.get(k, 0) if not isinstance(k, str) else self.cnt[k]
            if v and self.waited[eng].get(k, 0) < v:
                self.waited[eng][k] = v
                waits.append((k, v))
        self.ops[eng].append((waits, None, False, None))


class _Stop(Exception):
    pass


def build_program(nch=NCH, stop=99):
    nc = bass.Bass("TRN2", target_bir_lowering=False)
    x_d = nc.dram_tensor("x", [S, D], F32, kind="ExternalInput").ap()
    wst_d = nc.dram_tensor("wst", [NGRP, P, 4096], F32, kind="ExternalInput").ap()
    cvec_d = nc.dram_tensor("cvec", [P, NV], F32, kind="ExternalInput").ap()
    wa_d = nc.dram_tensor("lru_wa", [16, 64, 64], F32, kind="ExternalInput").ap()
    wx_d = nc.dram_tensor("lru_wx", [16, 64, 64], F32, kind="ExternalInput").ap()
    y_d = nc.dram_tensor("y", [S, D], F32, kind="ExternalOutput").ap()
    wbf_d = nc.dram_tensor("wbf", [NGRP, P, 4096], BF16, kind="Internal").ap()

    pr = Prog()
    with ExitStack() as es:
        def sb(name, shape, dt):
            return es.enter_context(nc.sbuf_tensor(name, shape, dt))

        xT = sb("xT", [P, KD, T], F32)
        xT_b = [Buf(f"xT{k}") for k in range(KD)]
        xin = [sb(f"xin{i}", [P, D], F32) for i in range(2)]
        xin_b = [Buf(f"xin{i}") for i in range(2)]
        yout = [sb(f"yout{i}", [P, D], F32) for i in range(2)]
        yout_b = [Buf(f"yout{i}") for i in range(2)]
        hT = sb("hT", [P, KD, T], BF16)
        hT_b = [Buf(f"hT{k}") for k in range(KD)]
        tmps = [sb(f"tmp{i}", [P, 520], F32) for i in range(N_TMP)]
        tmps_b = [Buf(f"tmp{i}") for i in range(N_TMP)]
        tbs = [sb(f"tb{i}", [P, T], BF16) for i in range(N_TB)]
        tbs_b = [Buf(f"tb{i}") for i in range(N_TB)]
        qa = sb("qa", [P, KD, T], BF16)
        qa_b = [Buf(f"qa{k}") for k in range(KD)]
        merged = sb("merged", [P, KD, T], BF16)
        merged_b = [Buf(f"mg{k}") for k in range(KD)]
        big = sb("big", [P, 6144], F32)
        big_bf = big.bitcast(BF16)
        act_b = [Buf(f"act{j}") for j in range(KF)]
        cgbuf = sb("cgbuf", [P, KD, 544], BF16)
        cg_b = [Buf(f"cg{j}") for j in range(KD)]
        dg = [sb(f"dg{i}", [P, 31, P], BF16) for i in range(2)]
        dg_b = [Buf(f"dg{i}") for i in range(2)]
        ring = [sb(f"ring{i}", [P, 4, KD, P], BF16) for i in range(NSLOT)]
        ring_b = [Buf(f"ring{i}") for i in range(NSLOT)]
        cv = sb("cv", [P, NV], F32)
        dv = sb("dv", [P, NDV], F32)
        const_b = Buf("const")
        ident_f = sb("ident_f", [P, P], F32)
        ones_f = sb("ones_f", [P, P], F32)
        ident_b = sb("ident_b", [P, P], BF16)
        bd_f = sb("bd_f", [P, KD, P], F32)
        bd_f_b = Buf("bd_f")
        bd_b = sb("bd_b", [P, 2, KD, P], BF16)
        xah = sb("xah", [P, KD, 3], F32)
        xah_b = Buf("xah")
        gh = sb("gh", [P, KF, 2], F32)
        gh_b = Buf("gh")
        hst = sb("hst", [P, KD], F32)
        hst_b = Buf("hst")
        lnm = sb("lnm", [P, T], F32)
        lnm_b = Buf("lnm")
        lnr = sb("lnr", [P, T], F32)
        lnr_b = Buf("lnr")
        banks = [es.enter_context(nc.psum_tensor(f"bank{i}", [P, T], F32)) for i in range(8)]
        banks_b = [Buf(f"bank{i}") for i in range(8)]
        wbf_b = [Buf(f"wbf{q}") for q in range(NGRP)]

        sems = {}
        for e in ("pe", "act", "dve", "pool"):
            sems[e] = es.enter_context(nc.semaphore(f"s_{e}"))
        dkeys = ([("ring", i) for i in range(NSLOT)] + [("xin", i) for i in range(2)]
                 + [("yout", i) for i in range(2)] + [("pre", i) for i in range(4)] + [("cst", 0), ("cst", 1)])
        for k in dkeys:
            sems[k] = es.enter_context(nc.semaphore(f"d_{k[0]}{k[1]}"))

        def cvc(name, i):
            o = _CV[name] + i
            return cv[:, o:o + 1]

        def dvc(name, i):
            o = _DV[name] + i
            return dv[:, o:o + 1]

        def ACT(out, in_, func, scale=1.0, bias=0.0, reads=(), writes=()):
            pr.op("act", lambda e: e.activation(out, in_, func, bias=bias, scale=scale), reads, writes)

        def TS(eng, out, in0, s1, s2, op0, op1, reads=(), writes=()):
            pr.op(eng, lambda e: e.tensor_scalar(out, in0, s1, s2, op0, op1), reads, writes)

        def TS1(eng, out, in0, s1, op0, reads=(), writes=()):
            pr.op(eng, lambda e: e.tensor_single_scalar(out, in0, s1, op0), reads, writes)

        def STT(eng, out, in0, scalar, in1, op0, op1, reads=(), writes=()):
            pr.op(eng, lambda e: e.scalar_tensor_tensor(out, in0, scalar, in1, op0, op1), reads, writes)

        def TT(eng, out, in0, in1, op, reads=(), writes=()):
            pr.op(eng, lambda e: e.tensor_tensor(out, in0, in1, op), reads, writes)

        def CP(eng, out, in_, reads=(), writes=()):
            pr.op(eng, lambda e: e.tensor_copy(out, in_), reads, writes)

        def MM(out, lhsT, rhs, start, stop, reads, writes, sig):
            pr.op("pe", lambda e: e.matmul(out, lhsT, rhs, start=start, stop=stop), reads, writes, inc=sig)

        def TR(out, in_, reads, writes, sig):
            pr.op("pe", lambda e: e.transpose(out, in_, ident_f[:]), reads, writes, inc=sig)

        st = {"bank": 0, "tmp": 0, "tb": 0, "grp": 0}

        def nbank():
            i = st["bank"]
            st["bank"] = (i + 1) % 6
            return banks[i], banks_b[i]

        def ntmp():
            i = st["tmp"]
            st["tmp"] = (i + 1) % N_TMP
            return tmps[i], tmps_b[i]

        def ntb():
            i = st["tb"]
            st["tb"] = (i + 1) % N_TB
            return tbs[i], tbs_b[i]

        wstate = {"slab": 0, "loaded": 0}

        def load_groups_upto(gidx):
            while wstate["loaded"] <= gidx:
                g = wstate["loaded"]
                q = g % NGRP
                slot = g % NSLOT
                pr.dma("sp", ("ring", slot), ring[slot][:].rearrange("p s k n -> p (s k n)"), wbf_d[q],
                       reads=[wbf_b[q]], writes=[ring_b[slot]])
                wstate["loaded"] += 1

        def next_slab():
            s_ = wstate["slab"]
            wstate["slab"] += 1
            g = s_ // 4
            load_groups_upto(min(g + NSLOT - 1, nch * NGRP - 1))
            slot = g % NSLOT
            return ring[slot][:, s_ % 4], ring_b[slot]

        def proj_group(rhs_list, rhs_bufs, bank=None, first=True, last=True):
            slab, slab_b = next_slab()
            if bank is None:
                bank = nbank()
            bk, bk_b = bank
            n = len(rhs_list)
            for k in range(n):
                MM(bk[:], slab[:, k, :], rhs_list[k], start=(first and k == 0), stop=(last and k == n - 1),
                   reads=[slab_b] + list(rhs_bufs), writes=[bk_b], sig=(k == n - 1))
            return bank

        pr.dma("sp", ("cst", 0), cv[:], cvec_d, writes=[const_b])
        pr.op("pool", lambda e: e.memset(ones_f[:], 1.0 / D), writes=[const_b])
        pr.op("pool", lambda e: e.memset(ident_f[:], 1.0), writes=[const_b])
        pr.op("pool", lambda e: e.affine_select(out=ident_f[:], in_=ident_f[:], pattern=[[-1, P]],
                                                compare_op=ALU.is_equal, fill=0.0, base=0,
                                                channel_multiplier=1), reads=[const_b], writes=[const_b])
        pr.op("pool", lambda e: e.memset(xah[:], 0.0), writes=[xah_b])
        pr.op("pool", lambda e: e.memset(gh[:], 0.0), writes=[gh_b])
        pr.op("pool", lambda e: e.memset(hst[:], 0.0), writes=[hst_b])
        pr.op("pool", lambda e: e.memset(cgbuf[:], 0.0), writes=cg_b)
        for q in range(NGRP):
            pr.dma("pool", ("pre", q % 4), wbf_d[q], wst_d[q], reads=([wbf_b[q - 4]] if q >= 4 else []),
                   writes=[wbf_b[q]])
        CP("dve", ident_b[:], ident_f[:], reads=[const_b], writes=[const_b])
        for gi, w_d in enumerate((wa_d, wx_d)):
            pr.op("pool", lambda e: e.memset(bd_f[:], 0.0), writes=[bd_f_b])
            wv = w_d.rearrange("(j two) d e -> two d j e", two=2)
            pr.dma("sp", ("cst", 1), bd_f[0:64, :, 0:64], wv[0], writes=[bd_f_b])
            pr.dma("sp", ("cst", 1), bd_f[64:128, :, 64:128], wv[1], writes=[bd_f_b])
            CP("dve", bd_b[:, gi], bd_f[:], reads=[bd_f_b], writes=[const_b])
        for nm, src in (("hba", "lba"), ("hbx", "lbx"), ("hbga", "bga"), ("hbgb", "bgb"),
                        ("hlng", "lng"), ("hlnb", "lnb")):
            TS1("dve", dv[:, _DV[nm]:_DV[nm] + 8], cv[:, _CV[src]:_CV[src] + 8], 0.5, ALU.mult,
                reads=[const_b], writes=[const_b])
        ACT(dv[:, _DV["e"]:_DV["e"] + 8], cv[:, _CV["lam"]:_CV["lam"] + 8], AF.Exp, scale=-1.0,
            reads=[const_b], writes=[const_b])
        ACT(dv[:, _DV["l"]:_DV["l"] + 8], dv[:, _DV["e"]:_DV["e"] + 8], AF.Ln, scale=1.0, bias=1.0,
            reads=[const_b], writes=[const_b])
        TS1("dve", dv[:, _DV["c1"]:_DV["c1"] + 8], dv[:, _DV["l"]:_DV["l"] + 8], -8.0, ALU.mult,
            reads=[const_b], writes=[const_b])
        TS1("dve", dv[:, _DV["hc1"]:_DV["hc1"] + 8], dv[:, _DV["l"]:_DV["l"] + 8], -4.0, ALU.mult,
            reads=[const_b], writes=[const_b])

        def rmsnorm_to_hT(gname):
            sbk, sbk_b = banks[6], banks_b[6]
            for k in range(KD):
                sq, sq_b = ntmp()
                ACT(sq[:, 0:T], xT[:, k, :], AF.Square, reads=[xT_b[k]], writes=[sq_b])
                MM(sbk[:], ones_f[:], sq[:, 0:T], start=(k == 0), stop=(k == KD - 1),
                   reads=[sq_b, const_b], writes=[sbk_b], sig=True)
            rs, rs_b = ntmp()
            ACT(rs[:, 0:T], sbk[:], AF.Sqrt, bias=EPS, reads=[sbk_b], writes=[rs_b])
            pr.op("dve", lambda e: e.reciprocal(rs[:, 0:T], rs[:, 0:T]), reads=[rs_b], writes=[rs_b])
            return rs, rs_b

        def chk(n):
            if n > stop:
                raise _Stop()

        for c in range(nch):
          try:
            t0 = c * T
            chk(1)
            for tt in range(4):
                xi, xi_b = xin[tt % 2], xin_b[tt % 2]
                pr.dma("sp", ("xin", tt % 2), xi[:], x_d[t0 + P * tt:t0 + P * (tt + 1), :], writes=[xi_b])
                for half in range(2):
                    bk, bk_b = nbank()
                    for kk in range(4):
                        k = 4 * half + kk
                        TR(bk[:, P * kk:P * (kk + 1)], xi[:, P * k:P * (k + 1)], reads=[xi_b, const_b],
                           writes=[bk_b], sig=(kk == 3))
                    ACT(xT[:, 4 * half:4 * half + 4, P * tt:P * (tt + 1)],
                        bk[:].rearrange("p (a b) -> p a b", a=4), AF.Identity,
                        reads=[bk_b], writes=xT_b[4 * half:4 * half + 4])
            chk(2)
            rs, rs_b = rmsnorm_to_hT("g_mix")
            for k in range(KD):
                STT("dve", hT[:, k, :], xT[:, k, :], cvc("g_mix", k), rs[:, 0:T], ALU.mult, ALU.mult,
                    reads=[xT_b[k], rs_b, const_b], writes=[hT_b[k]])
            hrhs = [hT[:, k, :] for k in range(KD)]

            chk(3)
            for j in range(KD):
                bcb = proj_group(hrhs, hT_b)
                bca = proj_group(hrhs, hT_b)
                tg, tg_b = ntmp()
                ACT(tg[:, 0:T], bcb[0][:], AF.Tanh, scale=0.5, reads=[bcb[1]], writes=[tg_b])
                STT("dve", cgbuf[:, j, 30:542], tg[:, 0:T], 1.0, bca[0][:], ALU.add, ALU.mult,
                    reads=[tg_b, bca[1]], writes=[cg_b[j]])
            chk(4)
            mbk, mbk_b = banks[6], banks_b[6]
            ebk, ebk_b = banks[7], banks_b[7]
            for j in range(KD):
                dgi = (c * KD + j) % 2
                for k in range(31):
                    o = _CV["cw"] + 8 * k + j
                    pr.op("pool", (lambda e, d=dg[dgi], k=k, o=o: e.tensor_scalar_mul(d[:, k, :], ident_b[:], cv[:, o:o + 1])),
                          reads=[const_b], writes=[dg_b[dgi]])
                bk, bk_b = nbank()
                for k in range(31):
                    MM(bk[:], dg[dgi][:, k, :], cgbuf[:, j, k:k + T], start=(k == 0), stop=(k == 30),
                       reads=[dg_b[dgi], cg_b[j]], writes=[bk_b], sig=(k == 30))
                CP("pool", cgbuf[:, j, 0:30], cgbuf[:, j, T:T + 30], reads=[cg_b[j]], writes=[cg_b[j]])
                cgc = big[:, T * j:T * (j + 1)]
                cgc_bufs = [act_b[2 * j], act_b[2 * j + 1]]
                ACT(cgc, bk[:], AF.Identity, scale=0.5, bias=cvc("cb", j), reads=[bk_b, const_b], writes=cgc_bufs)
                sq, sq_b = ntmp()
                TT("pool", sq[:, 0:T], cgc, cgc, ALU.mult, reads=cgc_bufs, writes=[sq_b])
                MM(mbk[:], ones_f[:], cgc, start=(j == 0), stop=(j == KD - 1), reads=cgc_bufs + [const_b],
                   writes=[mbk_b], sig=True)
                MM(ebk[:], ones_f[:], sq[:, 0:T], start=(j == 0), stop=(j == KD - 1), reads=[sq_b, const_b],
                   writes=[ebk_b], sig=True)
            chk(5)
            mean, mean_b = lnm, lnm_b
            ACT(mean[:, 0:T], mbk[:], AF.Identity, reads=[mbk_b], writes=[mean_b])
            lrs, lrs_b = lnr, lnr_b
            TT("dve", lrs[:, 0:T], mean[:, 0:T], mean[:, 0:T], ALU.mult, reads=[mean_b], writes=[lrs_b])
            TT("dve", lrs[:, 0:T], ebk[:], lrs[:, 0:T], ALU.subtract, reads=[ebk_b, lrs_b], writes=[lrs_b])
            TS1("dve", lrs[:, 0:T], lrs[:, 0:T], 0.0, ALU.max, reads=[lrs_b], writes=[lrs_b])
            ACT(lrs[:, 0:T], lrs[:, 0:T], AF.Sqrt, bias=EPS, reads=[lrs_b], writes=[lrs_b])
            pr.op("dve", lambda e, r=lrs: e.reciprocal(r[:, 0:T], r[:, 0:T]), reads=[lrs_b], writes=[lrs_b])
            for j in range(KD):
                cgc = big[:, T * j:T * (j + 1)]
                cgc_bufs = [act_b[2 * j], act_b[2 * j + 1]]
                nrm, nrm_b = ntmp()
                TT("dve", nrm[:, 0:T], cgc, mean[:, 0:T], ALU.subtract, reads=cgc_bufs + [mean_b], writes=[nrm_b])
                TT("pool", nrm[:, 0:T], nrm[:, 0:T], lrs[:, 0:T], ALU.mult, reads=[nrm_b, lrs_b], writes=[nrm_b])
                th, th_b = ntmp()
                ACT(th[:, 0:T], nrm[:, 0:T], AF.Tanh, scale=dvc("hlng", j), bias=dvc("hlnb", j),
                    reads=[nrm_b, const_b], writes=[th_b])
                TS("dve", nrm[:, 0:T], nrm[:, 0:T], cvc("lng", j), cvc("lnb", j), ALU.mult, ALU.add,
                   reads=[nrm_b, const_b], writes=[nrm_b])
                sj = big_bf[:, 8192 + T * j:8192 + T * (j + 1)]
                STT("dve", sj, th[:, 0:T], 1.0, nrm[:, 0:T], ALU.add, ALU.mult,
                    reads=[th_b, nrm_b], writes=[act_b[16 + j]])

            chk(6)
            def lru_tail(j, xc, xc_b, xcb, xcb_b, q, q_b):
                br = nbank()
                MM(br[0][:], bd_b[:, 0, j, :], xcb[:], start=True, stop=True, reads=[xcb_b, const_b],
                   writes=[br[1]], sig=True)
                bi = nbank()
                MM(bi[0][:], bd_b[:, 1, j, :], xcb[:], start=True, stop=True, reads=[xcb_b, const_b],
                   writes=[bi[1]], sig=True)
                tr_, tr_b = ntmp()
                ti_, ti_b = ntmp()
                aa, aa_b = ntmp()
                ACT(tr_[:, 0:T], br[0][:], AF.Tanh, scale=0.5, bias=dvc("hba", j), reads=[br[1], const_b], writes=[tr_b])
                ACT(ti_[:, 0:T], bi[0][:], AF.Tanh, scale=0.5, bias=dvc("hbx", j), reads=[bi[1], const_b], writes=[ti_b])
                ACT(aa[:, 0:T], tr_[:, 0:T], AF.Exp, scale=dvc("hc1", j), bias=dvc("hc1", j),
                    reads=[tr_b, const_b], writes=[aa_b])
                ACT(tr_[:, 0:T], tr_[:, 0:T], AF.Exp, scale=dvc("c1", j), bias=dvc("c1", j),
                    reads=[tr_b, const_b], writes=[tr_b])
                TS("dve", tr_[:, 0:T], tr_[:, 0:T], 1.0, -0.25, ALU.min, ALU.mult, reads=[tr_b], writes=[tr_b])
                STT("dve", ti_[:, 0:T], ti_[:, 0:T], 1.0, xc[:, 0:T], ALU.add, ALU.mult,
                    reads=[ti_b, xc_b], writes=[ti_b])
                ACT(tr_[:, 0:T], tr_[:, 0:T], AF.Sqrt, bias=0.25, reads=[tr_b], writes=[tr_b])
                TT("dve", ti_[:, 0:T], tr_[:, 0:T], ti_[:, 0:T], ALU.mult, reads=[tr_b, ti_b], writes=[ti_b])
                pr.op("dve", lambda e: e.tensor_tensor_scan(tr_[:, 0:T], aa[:, 0:T], ti_[:, 0:T], hst[:, j:j + 1],
                                                            ALU.mult, ALU.add),
                      reads=[aa_b, ti_b, hst_b], writes=[tr_b])
                CP("dve", hst[:, j:j + 1], tr_[:, T - 1:T], reads=[tr_b], writes=[hst_b])
                TT("dve", qa[:, j, :], q[:, 0:T], tr_[:, 0:T], ALU.mult, reads=[q_b, tr_b], writes=[qa_b[j]])

            pend = None
            for j in range(KD):
                bxa = proj_group(hrhs, hT_b)
                wk, wk_b = ntmp()
                CP("pool", wk[:, 0:3], xah[:, j, :], reads=[xah_b], writes=[wk_b])
                ACT(wk[:, 3:3 + T], bxa[0][:], AF.Identity, reads=[bxa[1]], writes=[wk_b])
                CP("pool", xah[:, j, :], wk[:, T:T + 3], reads=[wk_b], writes=[xah_b])
                xc, xc_b = ntmp()
                TS("dve", xc[:, 0:T], wk[:, 3:3 + T], cv[:, _CV["lcw"] + 24 + j:_CV["lcw"] + 25 + j], cvc("lcb", j),
                   ALU.mult, ALU.add, reads=[wk_b, const_b], writes=[xc_b])
                for k in (2, 1, 0):
                    o = _CV["lcw"] + 8 * k + j
                    STT("dve", xc[:, 0:T], wk[:, k:k + T], cv[:, o:o + 1], xc[:, 0:T], ALU.mult, ALU.add,
                        reads=[wk_b, xc_b, const_b], writes=[xc_b])
                xcb, xcb_b = ntb()
                CP("pool", xcb[:], xc[:, 0:T], reads=[xc_b], writes=[xcb_b])
                bga = proj_group(hrhs, hT_b)
                q, q_b = ntmp()
                ACT(q[:, 0:T], bga[0][:], AF.Square, scale=C_GELU, reads=[bga[1]], writes=[q_b])
                STT("dve", q[:, 0:T], q[:, 0:T], 1.0, bga[0][:], ALU.add, ALU.mult, reads=[q_b, bga[1]], writes=[q_b])
                ACT(q[:, 0:T], q[:, 0:T], AF.Tanh, scale=K_GELU, reads=[q_b], writes=[q_b])
                STT("dve", q[:, 0:T], q[:, 0:T], 1.0, bga[0][:], ALU.add, ALU.mult, reads=[q_b, bga[1]], writes=[q_b])
                if pend is not None:
                    lru_tail(*pend)
                pend = (j, xc, xc_b, xcb, xcb_b, q, q_b)
            lru_tail(*pend)

            chk(7)
            qrhs = [qa[:, k, :] for k in range(KD)]
            srhs = [big_bf[:, 8192 + T * k:8192 + T * (k + 1)] for k in range(KD)]
            s_bufs = [act_b[16 + k] for k in range(KD)]
            for m in range(KD):
                bsa = proj_group(hrhs, hT_b)
                bsb = proj_group(hrhs, hT_b)
                ta, ta_b = ntmp()
                tb_, tb_b = ntmp()
                ACT(ta[:, 0:T], bsa[0][:], AF.Tanh, scale=0.5, bias=dvc("hbga", m), reads=[bsa[1], const_b], writes=[ta_b])
                ACT(tb_[:, 0:T], bsb[0][:], AF.Tanh, scale=0.5, bias=dvc("hbgb", m), reads=[bsb[1], const_b], writes=[tb_b])
                bA = proj_group(qrhs, qa_b)
                bB = proj_group(srhs, s_bufs)
                STT("dve", ta[:, 0:T], ta[:, 0:T], 1.0, bA[0][:], ALU.add, ALU.mult, reads=[ta_b, bA[1]], writes=[ta_b])
                STT("dve", tb_[:, 0:T], tb_[:, 0:T], 1.0, bB[0][:], ALU.add, ALU.mult, reads=[tb_b, bB[1]], writes=[tb_b])
                TT("pool", merged[:, m, :], ta[:, 0:T], tb_[:, 0:T], ALU.add, reads=[ta_b, tb_b], writes=[merged_b[m]])
            chk(8)
            mrhs = [merged[:, k, :] for k in range(KD)]
            for m in range(KD):
                bo = proj_group(mrhs, merged_b)
                STT("dve", xT[:, m, :], bo[0][:], 0.25, xT[:, m, :], ALU.mult, ALU.add,
                    reads=[bo[1], xT_b[m]], writes=[xT_b[m]])

            chk(9)
            rs, rs_b = rmsnorm_to_hT("g_ffn")
            for k in range(KD):
                STT("dve", hT[:, k, :], xT[:, k, :], cvc("g_ffn", k), rs[:, 0:T], ALU.mult, ALU.mult,
                    reads=[xT_b[k], rs_b, const_b], writes=[hT_b[k]])
            for j in range(KF):
                bg = proj_group(hrhs, hT_b)
                bv = proj_group(hrhs, hT_b)
                wk, wk_b = ntmp()
                CP("pool", wk[:, 0:2], gh[:, j, :], reads=[gh_b], writes=[wk_b])
                ACT(wk[:, 2:2 + T], bg[0][:], AF.Identity, reads=[bg[1]], writes=[wk_b])
                CP("pool", gh[:, j, :], wk[:, T:T + 2], reads=[wk_b], writes=[gh_b])
                gc, gc_b = ntmp()
                o2 = _CV["fw"] + 24 * 2 + j
                TS("dve", gc[:, 0:T], wk[:, 2:2 + T], cv[:, o2:o2 + 1], cvc("fb", j), ALU.mult, ALU.add,
                   reads=[wk_b, const_b], writes=[gc_b])
                for k in (1, 0):
                    o = _CV["fw"] + 24 * k + j
                    STT("dve", gc[:, 0:T], wk[:, k:k + T], cv[:, o:o + 1], gc[:, 0:T], ALU.mult, ALU.add,
                        reads=[wk_b, gc_b, const_b], writes=[gc_b])
                pp, pp_b = ntmp()
                ACT(pp[:, 0:T], gc[:, 0:T], AF.Square, scale=C_GELU, reads=[gc_b], writes=[pp_b])
                STT("dve", pp[:, 0:T], pp[:, 0:T], 1.0, gc[:, 0:T], ALU.add, ALU.mult, reads=[pp_b, gc_b], writes=[pp_b])
                ACT(pp[:, 0:T], pp[:, 0:T], AF.Tanh, scale=K_GELU, reads=[pp_b], writes=[pp_b])
                TT("dve", gc[:, 0:T], gc[:, 0:T], bv[0][:], ALU.mult, reads=[gc_b, bv[1]], writes=[gc_b])
                STT("dve", big_bf[:, T * j:T * (j + 1)], pp[:, 0:T], 1.0, gc[:, 0:T], ALU.add, ALU.mult,
                    reads=[pp_b, gc_b], writes=[act_b[j]])
            chk(11)
            for m in range(KD):
                bk = nbank()
                for th3 in range(3):
                    arhs = [big_bf[:, T * (8 * th3 + k):T * (8 * th3 + k + 1)] for k in range(KD)]
                    proj_group(arhs, act_b[8 * th3:8 * th3 + 8], bank=bk, first=(th3 == 0), last=(th3 == 2))
                STT("dve", xT[:, m, :], bk[0][:], 0.5, xT[:, m, :], ALU.mult, ALU.add,
                    reads=[bk[1], xT_b[m]], writes=[xT_b[m]])

            chk(12)
            rs, rs_b = rmsnorm_to_hT("g_fin")
            for k in range(KD):
                STT("dve", xT[:, k, :], xT[:, k, :], cvc("g_fin", k), rs[:, 0:T], ALU.mult, ALU.mult,
                    reads=[xT_b[k], rs_b, const_b], writes=[xT_b[k]])
            for tt in range(4):
                yo, yo_b = yout[tt % 2], yout_b[tt % 2]
                for half in range(2):
                    bk, bk_b = nbank()
                    for kk in range(4):
                        k = 4 * half + kk
                        TR(bk[:, P * kk:P * (kk + 1)], xT[:, k, P * tt:P * (tt + 1)], reads=[xT_b[k], const_b],
                           writes=[bk_b], sig=(kk == 3))
                    if half == 0:
                        ACT(yo[:, 0:T], bk[:], AF.Identity, reads=[bk_b], writes=[yo_b])
                    else:
                        CP("dve", yo[:, T:2 * T], bk[:], reads=[bk_b], writes=[yo_b])
                pr.dma("act", ("yout", tt % 2), y_d[t0 + P * tt:t0 + P * (tt + 1), :], yo[:], reads=[yo_b])

          except _Stop:
            break

        pr.wait_all("act", [("yout", 0), ("yout", 1)])
        pr.wait_all("sp", [("yout", 0), ("yout", 1)])

        def replay(name, eng):
            for waits, emit, inc, dsem in pr.ops[name]:
                for k, v in waits:
                    eng.wait_ge(sems[k], v)
                if emit is None:
                    continue
                ins = emit(eng)
                if dsem is not None:
                    ins.then_inc(sems[dsem], 16)
                elif inc:
                    ins.then_inc(sems[name], 1)

        with nc.Block() as block:
            @block.sync
            def _(e):
                replay("sp", e)

            @block.tensor
            def _(e):
                replay("pe", e)

            @block.scalar
            def _(e):
                replay("act", e)

            @block.vector
            def _(e):
                replay("dve", e)

            @block.gpsimd
            def _(e):
                replay("pool", e)
    return nc


def _fm(v):
    v = np.asarray(v, np.float32)
    return np.ascontiguousarray(v.reshape(-1, P).T)


def _slab(W, r0, c0):
    return W[r0:r0 + 1024, c0:c0 + P].reshape(KD, P, P).transpose(1, 0, 2)


def _weight_stream(w_in, w_a, w_b, w_out, w_up, w_down):
    slabs = []
    for j in range(KD):
        slabs.append(_slab(w_in, 0, 3072 + P * j))
        slabs.append(_slab(w_in, 0, 2048 + P * j))
    for j in range(KD):
        slabs.append(_slab(w_in, 0, P * j))
        slabs.append(_slab(w_in, 0, 1024 + P * j))
    for m in range(KD):
        slabs.append(_slab(w_in, 0, 4096 + P * m))
        slabs.append(_slab(w_in, 0, 5120 + P * m))
        slabs.append(_slab(w_a, 0, P * m))
        slabs.append(_slab(w_b, 0, P * m))
    for m in range(KD):
        slabs.append(_slab(w_out, 0, P * m))
    for j in range(KF):
        slabs.append(_slab(w_up, 0, P * j))
        slabs.append(_slab(w_up, 0, DFF + P * j))
    for m in range(KD):
        for th3 in range(3):
            slabs.append(_slab(w_down, 1024 * th3, P * m))
    assert len(slabs) == 4 * NGRP
    st = np.stack(slabs).reshape(NGRP, 4, P, KD * P).transpose(0, 2, 1, 3)
    return np.ascontiguousarray(st.reshape(NGRP, P, 4096), dtype=np.float32)


def _cvec(inp):
    cvv = np.zeros((P, NV), np.float32)

    def put(name, arr):
        arr = np.asarray(arr, np.float32)
        cvv[:, _CV[name]:_CV[name] + arr.shape[1]] = arr
    put("g_mix", _fm(inp["g_mix"][0]))
    put("lcw", np.concatenate([_fm(inp["lru_conv_w"][0, k]) for k in range(4)], axis=1))
    put("lcb", _fm(inp["lru_conv_b"][0]))
    put("lba", _fm(inp["lru_ba"][0]))
    put("lbx", _fm(inp["lru_bx"][0]))
    put("lam", _fm(inp["lru_lambda"][0]))
    put("cw", np.concatenate([_fm(inp["conf_dw_w"][0, k]) for k in range(31)], axis=1))
    put("cb", _fm(inp["conf_dw_b"][0]))
    put("lng", _fm(inp["conf_ln_g"][0]))
    put("lnb", _fm(inp["conf_ln_b"][0]))
    put("bga", _fm(inp["b_gate"][0, :D]))
    put("bgb", _fm(inp["b_gate"][0, D:]))
    put("g_ffn", _fm(inp["g_ffn"][0]))
    put("fw", np.concatenate([_fm(inp["ffn_dw_w"][0, k]) for k in range(3)], axis=1))
    put("fb", _fm(inp["ffn_dw_b"][0]))
    put("g_fin", _fm(inp["g_final"]))
    return cvv


_NC_CACHE = {}


def kernel(**inputs):
    inp = {k: np.asarray(v) for k, v in inputs.items()}
    if "nc" not in _NC_CACHE:
        _NC_CACHE["nc"] = build_program()
    nc = _NC_CACHE["nc"]
    wst = _weight_stream(inp["w_in"][0], inp["w_lru_out"][0], inp["w_conf_out"][0], inp["w_out"][0],
                         inp["w_up"][0], inp["w_down"][0])
    cvv = _cvec(inp)
    wa = np.ascontiguousarray(inp["lru_wa"][0], dtype=np.float32)
    wx = np.ascontiguousarray(inp["lru_wx"][0], dtype=np.float32)
    x = np.asarray(inp["x"], np.float32)
    in_maps = [{"x": np.ascontiguousarray(x[b]), "wst": wst, "cvec": cvv, "lru_wa": wa, "lru_wx": wx}
               for b in range(8)]
    res = run_bass_kernel_spmd(nc, in_maps, core_ids=list(range(8)))
    return np.stack([np.asarray(res.results[b]["y"], dtype=np.float32).reshape(S, D) for b in range(8)])
```

```python
import numpy as np
from contextlib import ExitStack
import concourse.bass as bass
import concourse.mybir as mybir
from concourse.bass_utils import run_bass_kernel_spmd

F32 = mybir.dt.float32
BF16 = mybir.dt.bfloat16
AF = mybir.ActivationFunctionType
ALU = mybir.AluOpType

P = 128
S = 4096
D = 1024
T = 512
NCH = S // T
KD = 8
DFF = 3072
KF = 24
EPS = 1e-6
K_GELU = 0.7978845608028654
C_GELU = 0.21145921
NGRP = 36
NSLOT = 4
N_TMP = 20
N_TB = 4

_CV = {}
def _mk_cv():
    o = 0
    for name, n in (("g_mix", 8), ("lcw", 32), ("lcb", 8), ("lba", 8), ("lbx", 8), ("lam", 8),
                    ("cw", 248), ("cb", 8), ("lng", 8), ("lnb", 8), ("bga", 8), ("bgb", 8),
                    ("g_ffn", 8), ("fw", 72), ("fb", 24), ("g_fin", 8)):
        _CV[name] = o
        o += n
    return o
NV = _mk_cv()
_DV = {n: 8 * i for i, n in enumerate(("hba", "hbx", "hbga", "hbgb", "c1", "hc1", "hlng", "hlnb", "e", "l"))}
NDV = 80


class Buf:
    __slots__ = ("name", "w", "r")

    def __init__(self, name):
        self.name = name
        self.w = None
        self.r = {}


class Prog:
    ENGS = ("pe", "act", "dve", "pool", "sp")

    def __init__(self):
        self.ops = {e: [] for e in self.ENGS}
        self.cnt = {e: 0 for e in self.ENGS}
        self.waited = {e: {} for e in self.ENGS}
        self.dcnt = {}
        self.tag = ""
        self.tags = {e: [] for e in self.ENGS}

    def _deps(self, eng, reads, writes):
        need = {}

        def add(k, v):
            if k == "pe" and eng == "pe":
                return
            if need.get(k, 0) < v:
                need[k] = v
        for b in reads:
            if b.w is not None:
                add(*b.w)
        for b in writes:
            if b.w is not None:
                add(*b.w)
            for k, v in b.r.items():
                add(k, v)
        wd = self.waited[eng]
        waits = []
        for k, v in need.items():
            if wd.get(k, 0) < v:
                wd[k] = v
                waits.append((k, v))
        return waits

    def _mark(self, tok, reads, writes):
        k, v = tok
        for b in reads:
            if b.r.get(k, 0) < v:
                b.r[k] = v
        for b in writes:
            b.w = tok
            b.r = {}

    def op(self, eng, emit, reads=(), writes=(), inc=True):
        waits = self._deps(eng, reads, writes)
        if inc:
            self.cnt[eng] += 1
            tok = (eng, self.cnt[eng])
        else:
            tok = (eng, self.cnt[eng] + 1)
        self._mark(tok, reads, writes)
        self.ops[eng].append((waits, emit, inc, None))
        self.tags[eng].append(self.tag)

    def dma(self, eng, semkey, out, in_, reads=(), writes=()):
        waits = self._deps(eng, reads, writes)
        self.dcnt[semkey] = self.dcnt.get(semkey, 0) + 16
        tok = (semkey, self.dcnt[semkey])
        self._mark(tok, reads, writes)
        self.ops[eng].append((waits, (lambda e, o=out, i=in_: e.dma_start(out=o, in_=i)), False, semkey))

    def wait_all(self, eng, keys):
        waits = []
        for k in keys:
            v = self.dcnt.get(k, 0) if not isinstance(k, str) else self.cnt[k]
            if v and self.waited[eng].get(k, 0) < v:
                self.waited[eng][k] = v
                waits.append((k, v))
        self.ops[eng].append((waits, None, False, None))


class _Stop(Exception):
    pass


def build_program(nch=NCH, stop=99):
    nc = bass.Bass("TRN2", target_bir_lowering=False)
    x_d = nc.dram_tensor("x", [S, D], F32, kind="ExternalInput").ap()
    wst_d = nc.dram_tensor("wst", [NGRP, P, 4096], F32, kind="ExternalInput").ap()
    cvec_d = nc.dram_tensor("cvec", [P, NV], F32, kind="ExternalInput").ap()
    wa_d = nc.dram_tensor("lru_wa", [16, 64, 64], F32, kind="ExternalInput").ap()
    wx_d = nc.dram_tensor("lru_wx", [16, 64, 64], F32, kind="ExternalInput").ap()
    y_d = nc.dram_tensor("y", [S, D], F32, kind="ExternalOutput").ap()
    wbf_d = nc.dram_tensor("wbf", [NGRP, P, 4096], BF16, kind="Internal").ap()
    dgs_d = nc.dram_tensor("dgs", [KD, P, 31 * P], BF16, kind="Internal").ap()

    pr = Prog()
    with ExitStack() as es:
        def sb(name, shape, dt):
            return es.enter_context(nc.sbuf_tensor(name, shape, dt))

        xT = sb("xT", [P, KD, T], F32)
        xT_b = [Buf(f"xT{k}") for k in range(KD)]
        xin = [sb(f"xin{i}", [P, D], F32) for i in range(2)]
        xin_b = [Buf(f"xin{i}") for i in range(2)]
        yout = [sb(f"yout{i}", [P, D], F32) for i in range(2)]
        yout_b = [Buf(f"yout{i}") for i in range(2)]
        hT = sb("hT", [P, KD, T], BF16)
        hT_b = [Buf(f"hT{k}") for k in range(KD)]
        tmps = [sb(f"tmp{i}", [P, 520], F32) for i in range(N_TMP)]
        tmps_b = [Buf(f"tmp{i}") for i in range(N_TMP)]
        tbs = [sb(f"tb{i}", [P, T], BF16) for i in range(N_TB)]
        tbs_b = [Buf(f"tb{i}") for i in range(N_TB)]
        qa = sb("qa", [P, KD, T], BF16)
        qa_b = [Buf(f"qa{k}") for k in range(KD)]
        merged = sb("merged", [P, KD, T], BF16)
        merged_b = [Buf(f"mg{k}") for k in range(KD)]
        big = sb("big", [P, 6144], F32)
        big_bf = big.bitcast(BF16)
        act_b = [Buf(f"act{j}") for j in range(KF)]
        cgbuf = sb("cgbuf", [P, KD, 544], BF16)
        cg_b = [Buf(f"cg{j}") for j in range(KD)]
        dg = [sb(f"dg{i}", [P, 31, P], BF16) for i in range(2)]
        dg_b = [Buf(f"dg{i}") for i in range(2)]
        ring = [sb(f"ring{i}", [P, 4, KD, P], BF16) for i in range(NSLOT)]
        ring_b = [Buf(f"ring{i}") for i in range(NSLOT)]
        cv = sb("cv", [P, NV], F32)
        dv = sb("dv", [P, NDV], F32)
        const_b = Buf("const")
        ident_f = sb("ident_f", [P, P], F32)
        ones_f = sb("ones_f", [P, P], F32)
        ident_b = sb("ident_b", [P, P], BF16)
        bd_f = sb("bd_f", [P, KD, P], F32)
        bd_f_b = Buf("bd_f")
        bd_b = sb("bd_b", [P, 2, KD, P], BF16)
        xah = sb("xah", [P, KD, 3], F32)
        xah_b = Buf("xah")
        gh = sb("gh", [P, KF, 2], F32)
        gh_b = Buf("gh")
        hst = sb("hst", [P, KD], F32)
        hst_b = Buf("hst")
        lnm = sb("lnm", [P, T], F32)
        lnm_b = Buf("lnm")
        lnr = sb("lnr", [P, T], F32)
        lnr_b = Buf("lnr")
        banks = [es.enter_context(nc.psum_tensor(f"bank{i}", [P, T], F32)) for i in range(8)]
        banks_b = [Buf(f"bank{i}") for i in range(8)]
        wbf_b = [Buf(f"wbf{q}") for q in range(NGRP)]
        dgs_b = [Buf(f"dgs{j}") for j in range(KD)]

        sems = {}
        for e in ("pe", "act", "dve", "pool"):
            sems[e] = es.enter_context(nc.semaphore(f"s_{e}"))
        dkeys = ([("ring", i) for i in range(NSLOT)] + [("xin", i) for i in range(2)]
                 + [("yout", i) for i in range(2)] + [("pre", i) for i in range(4)] + [("cst", 0), ("cst", 1)]
                 + [("dg", i) for i in range(2)] + [("dgs", i) for i in range(2)])
        for k in dkeys:
            sems[k] = es.enter_context(nc.semaphore(f"d_{k[0]}{k[1]}"))

        def cvc(name, i):
            o = _CV[name] + i
            return cv[:, o:o + 1]

        def dvc(name, i):
            o = _DV[name] + i
            return dv[:, o:o + 1]

        def ACT(out, in_, func, scale=1.0, bias=0.0, reads=(), writes=()):
            pr.op("act", lambda e: e.activation(out, in_, func, bias=bias, scale=scale), reads, writes)

        def TS(eng, out, in0, s1, s2, op0, op1, reads=(), writes=()):
            pr.op(eng, lambda e: e.tensor_scalar(out, in0, s1, s2, op0, op1), reads, writes)

        def TS1(eng, out, in0, s1, op0, reads=(), writes=()):
            pr.op(eng, lambda e: e.tensor_single_scalar(out, in0, s1, op0), reads, writes)

        def STT(eng, out, in0, scalar, in1, op0, op1, reads=(), writes=()):
            pr.op(eng, lambda e: e.scalar_tensor_tensor(out, in0, scalar, in1, op0, op1), reads, writes)

        def TT(eng, out, in0, in1, op, reads=(), writes=()):
            pr.op(eng, lambda e: e.tensor_tensor(out, in0, in1, op), reads, writes)

        def CP(eng, out, in_, reads=(), writes=()):
            pr.op(eng, lambda e: e.tensor_copy(out, in_), reads, writes)

        def MM(out, lhsT, rhs, start, stop, reads, writes, sig):
            pr.op("pe", lambda e: e.matmul(out, lhsT, rhs, start=start, stop=stop), reads, writes, inc=sig)

        def TR(out, in_, reads, writes, sig):
            pr.op("pe", lambda e: e.transpose(out, in_, ident_f[:]), reads, writes, inc=sig)

        st = {"bank": 0, "tmp": 0, "tb": 0, "grp": 0}

        def nbank():
            i = st["bank"]
            st["bank"] = (i + 1) % 6
            return banks[i], banks_b[i]

        def ntmp():
            i = st["tmp"]
            st["tmp"] = (i + 1) % N_TMP
            return tmps[i], tmps_b[i]

        def ntb():
            i = st["tb"]
            st["tb"] = (i + 1) % N_TB
            return tbs[i], tbs_b[i]

        wstate = {"slab": 0, "loaded": 0}

        def load_groups_upto(gidx):
            while wstate["loaded"] <= gidx:
                g = wstate["loaded"]
                q = g % NGRP
                slot = g % NSLOT
                pr.dma("sp", ("ring", slot), ring[slot][:].rearrange("p s k n -> p (s k n)"), wbf_d[q],
                       reads=[wbf_b[q]], writes=[ring_b[slot]])
                wstate["loaded"] += 1

        def next_slab():
            s_ = wstate["slab"]
            wstate["slab"] += 1
            g = s_ // 4
            load_groups_upto(min(g + NSLOT - 1, nch * NGRP - 1))
            slot = g % NSLOT
            return ring[slot][:, s_ % 4], ring_b[slot]

        def proj_group(rhs_list, rhs_bufs, bank=None, first=True, last=True):
            slab, slab_b = next_slab()
            if bank is None:
                bank = nbank()
            bk, bk_b = bank
            n = len(rhs_list)
            for k in range(n):
                MM(bk[:], slab[:, k, :], rhs_list[k], start=(first and k == 0), stop=(last and k == n - 1),
                   reads=[slab_b] + list(rhs_bufs), writes=[bk_b], sig=(k == n - 1))
            return bank

        pr.dma("sp", ("cst", 0), cv[:], cvec_d, writes=[const_b])
        pr.op("pool", lambda e: e.memset(ones_f[:], 1.0 / D), writes=[const_b])
        pr.op("pool", lambda e: e.memset(ident_f[:], 1.0), writes=[const_b])
        pr.op("pool", lambda e: e.affine_select(out=ident_f[:], in_=ident_f[:], pattern=[[-1, P]],
                                                compare_op=ALU.is_equal, fill=0.0, base=0,
                                                channel_multiplier=1), reads=[const_b], writes=[const_b])
        pr.op("pool", lambda e: e.memset(xah[:], 0.0), writes=[xah_b])
        pr.op("pool", lambda e: e.memset(gh[:], 0.0), writes=[gh_b])
        pr.op("pool", lambda e: e.memset(hst[:], 0.0), writes=[hst_b])
        pr.op("pool", lambda e: e.memset(cgbuf[:], 0.0), writes=cg_b)
        for q in range(NGRP):
            pr.dma("pool", ("pre", q % 4), wbf_d[q], wst_d[q], reads=([wbf_b[q - 4]] if q >= 4 else []),
                   writes=[wbf_b[q]])
        CP("dve", ident_b[:], ident_f[:], reads=[const_b], writes=[const_b])
        for gi, w_d in enumerate((wa_d, wx_d)):
            pr.op("pool", lambda e: e.memset(bd_f[:], 0.0), writes=[bd_f_b])
            wv = w_d.rearrange("(j two) d e -> two d j e", two=2)
            pr.dma("sp", ("cst", 1), bd_f[0:64, :, 0:64], wv[0], writes=[bd_f_b])
            pr.dma("sp", ("cst", 1), bd_f[64:128, :, 64:128], wv[1], writes=[bd_f_b])
            CP("dve", bd_b[:, gi], bd_f[:], reads=[bd_f_b], writes=[const_b])
        for j in range(KD):
            for k in range(31):
                o = _CV["cw"] + 8 * k + j
                pr.op("dve", (lambda e, d=dg[j % 2], k=k, o=o: e.tensor_scalar_mul(d[:, k, :], ident_b[:], cv[:, o:o + 1])),
                      reads=[const_b], writes=[dg_b[j % 2]])
            pr.dma("sp", ("dgs", j % 2), dgs_d[j], dg[j % 2][:].rearrange("p k n -> p (k n)"),
                   reads=[dg_b[j % 2]], writes=[dgs_b[j]])
        for nm, src in (("hba", "lba"), ("hbx", "lbx"), ("hbga", "bga"), ("hbgb", "bgb"),
                        ("hlng", "lng"), ("hlnb", "lnb")):
            TS1("dve", dv[:, _DV[nm]:_DV[nm] + 8], cv[:, _CV[src]:_CV[src] + 8], 0.5, ALU.mult,
                reads=[const_b], writes=[const_b])
        ACT(dv[:, _DV["e"]:_DV["e"] + 8], cv[:, _CV["lam"]:_CV["lam"] + 8], AF.Exp, scale=-1.0,
            reads=[const_b], writes=[const_b])
        ACT(dv[:, _DV["l"]:_DV["l"] + 8], dv[:, _DV["e"]:_DV["e"] + 8], AF.Ln, scale=1.0, bias=1.0,
            reads=[const_b], writes=[const_b])
        TS1("dve", dv[:, _DV["c1"]:_DV["c1"] + 8], dv[:, _DV["l"]:_DV["l"] + 8], -8.0, ALU.mult,
            reads=[const_b], writes=[const_b])
        TS1("dve", dv[:, _DV["hc1"]:_DV["hc1"] + 8], dv[:, _DV["l"]:_DV["l"] + 8], -4.0, ALU.mult,
            reads=[const_b], writes=[const_b])

        def norm_acc(k, first, last):
            sq, sq_b = ntmp()
            ACT(sq[:, 0:T], xT[:, k, :], AF.Square, reads=[xT_b[k]], writes=[sq_b])
            MM(banks[6][:], ones_f[:], sq[:, 0:T], start=first, stop=last,
               reads=[sq_b, const_b], writes=[banks_b[6]], sig=True)

        def norm_finish():
            rs, rs_b = ntmp()
            ACT(rs[:, 0:T], banks[6][:], AF.Ln, bias=EPS, reads=[banks_b[6]], writes=[rs_b])
            ACT(rs[:, 0:T], rs[:, 0:T], AF.Exp, scale=-0.5, reads=[rs_b], writes=[rs_b])
            return rs, rs_b

        names = {1: "S0", 2: "norm1", 3: "mix", 4: "conv31", 5: "LN", 6: "brA", 7: "merge", 8: "Wout", 9: "ffn_norm_up",
                 11: "down", 12: "final"}

        def chk(n):
            if n > stop:
                raise _Stop()
            pr.tag = names[n]

        for c in range(nch):
          try:
            t0 = c * T
            chk(1)
            for tt in range(4):
                xi, xi_b = xin[tt % 2], xin_b[tt % 2]
                pr.dma("sp", ("xin", tt % 2), xi[:], x_d[t0 + P * tt:t0 + P * (tt + 1), :], writes=[xi_b])
                for half in range(2):
                    bk, bk_b = nbank()
                    for kk in range(4):
                        k = 4 * half + kk
                        TR(bk[:, P * kk:P * (kk + 1)], xi[:, P * k:P * (k + 1)], reads=[xi_b, const_b],
                           writes=[bk_b], sig=(kk == 3))
                    ACT(xT[:, 4 * half:4 * half + 4, P * tt:P * (tt + 1)],
                        bk[:].rearrange("p (a b) -> p a b", a=4), AF.Identity,
                        reads=[bk_b], writes=xT_b[4 * half:4 * half + 4])
            chk(2)
            for k in range(KD):
                norm_acc(k, k == 0, k == KD - 1)
            rs, rs_b = norm_finish()
            for k in range(KD):
                STT("dve", hT[:, k, :], xT[:, k, :], cvc("g_mix", k), rs[:, 0:T], ALU.mult, ALU.mult,
                    reads=[xT_b[k], rs_b, const_b], writes=[hT_b[k]])
            hrhs = [hT[:, k, :] for k in range(KD)]

            chk(3)
            mbk, mbk_b = banks[6], banks_b[6]
            ebk, ebk_b = banks[7], banks_b[7]

            def b1(j):
                bcb = proj_group(hrhs, hT_b)
                bca = proj_group(hrhs, hT_b)
                tg, tg_b = ntmp()
                ACT(tg[:, 0:T], bcb[0][:], AF.Tanh, scale=0.5, reads=[bcb[1]], writes=[tg_b])
                STT("dve", cgbuf[:, j, 30:542], tg[:, 0:T], 1.0, bca[0][:], ALU.add, ALU.mult,
                    reads=[tg_b, bca[1]], writes=[cg_b[j]])

            def a_front(j):
                bxa = proj_group(hrhs, hT_b)
                bga = proj_group(hrhs, hT_b)
                wk, wk_b = ntmp()
                CP("pool", wk[:, 0:3], xah[:, j, :], reads=[xah_b], writes=[wk_b])
                ACT(wk[:, 3:3 + T], bxa[0][:], AF.Identity, reads=[bxa[1]], writes=[wk_b])
                xc, xc_b = ntmp()
                ACT(xc[:, 0:T], bxa[0][:], AF.Identity, scale=cv[:, _CV["lcw"] + 24 + j:_CV["lcw"] + 25 + j],
                    bias=cvc("lcb", j), reads=[bxa[1], const_b], writes=[xc_b])
                CP("pool", xah[:, j, :], wk[:, T:T + 3], reads=[wk_b], writes=[xah_b])
                q, q_b = ntmp()
                ACT(q[:, 0:T], bga[0][:], AF.Gelu_apprx_tanh, reads=[bga[1]], writes=[q_b])
                for k in (2, 1, 0):
                    o = _CV["lcw"] + 8 * k + j
                    STT("dve", xc[:, 0:T], wk[:, k:k + T], cv[:, o:o + 1], xc[:, 0:T], ALU.mult, ALU.add,
                        reads=[wk_b, xc_b, const_b], writes=[xc_b])
                xcb, xcb_b = ntb()
                ACT(xcb[:], xc[:, 0:T], AF.Identity, reads=[xc_b], writes=[xcb_b])
                return (j, xc, xc_b, xcb, xcb_b, q, q_b)

            def dg_load(g):
                pr.dma("pool", ("dg", g % 2), dg[g % 2][:].rearrange("p k n -> p (k n)"), dgs_d[g % KD],
                       reads=[dgs_b[g % KD]], writes=[dg_b[g % 2]])

            def conv(j):
                dgi = (c * KD + j) % 2
                if c == 0 and j == 0:
                    dg_load(0)
                bk, bk_b = nbank()
                for k in range(31):
                    MM(bk[:], dg[dgi][:, k, :], cgbuf[:, j, k:k + T], start=(k == 0), stop=(k == 30),
                       reads=[dg_b[dgi], cg_b[j]], writes=[bk_b], sig=(k == 30))
                if c * KD + j + 1 < nch * KD:
                    dg_load(c * KD + j + 1)
                CP("pool", cgbuf[:, j, 0:30], cgbuf[:, j, T:T + 30], reads=[cg_b[j]], writes=[cg_b[j]])
                cgc = big[:, T * j:T * (j + 1)]
                cgc_bufs = [act_b[2 * j], act_b[2 * j + 1]]
                ACT(cgc, bk[:], AF.Identity, scale=0.5, bias=cvc("cb", j), reads=[bk_b, const_b], writes=cgc_bufs)
                sq, sq_b = ntmp()
                TT("pool", sq[:, 0:T], cgc, cgc, ALU.mult, reads=cgc_bufs, writes=[sq_b])
                MM(mbk[:], ones_f[:], cgc, start=(j == 0), stop=(j == KD - 1), reads=cgc_bufs + [const_b],
                   writes=[mbk_b], sig=True)
                MM(ebk[:], ones_f[:], sq[:, 0:T], start=(j == 0), stop=(j == KD - 1), reads=[sq_b, const_b],
                   writes=[ebk_b], sig=True)

            def lru_tail(j, xc, xc_b, xcb, xcb_b, q, q_b):
                br = nbank()
                MM(br[0][:], bd_b[:, 0, j, :], xcb[:], start=True, stop=True, reads=[xcb_b, const_b],
                   writes=[br[1]], sig=True)
                bi = nbank()
                MM(bi[0][:], bd_b[:, 1, j, :], xcb[:], start=True, stop=True, reads=[xcb_b, const_b],
                   writes=[bi[1]], sig=True)
                tr_, tr_b = ntmp()
                ti_, ti_b = ntmp()
                aa, aa_b = ntmp()
                ACT(tr_[:, 0:T], br[0][:], AF.Tanh, scale=0.5, bias=dvc("hba", j), reads=[br[1], const_b], writes=[tr_b])
                ACT(ti_[:, 0:T], bi[0][:], AF.Tanh, scale=0.5, bias=dvc("hbx", j), reads=[bi[1], const_b], writes=[ti_b])
                ACT(aa[:, 0:T], tr_[:, 0:T], AF.Exp, scale=dvc("hc1", j), bias=dvc("hc1", j),
                    reads=[tr_b, const_b], writes=[aa_b])
                ACT(tr_[:, 0:T], tr_[:, 0:T], AF.Exp, scale=dvc("c1", j), bias=dvc("c1", j),
                    reads=[tr_b, const_b], writes=[tr_b])
                TS("dve", tr_[:, 0:T], tr_[:, 0:T], 1.0, -0.25, ALU.min, ALU.mult, reads=[tr_b], writes=[tr_b])
                STT("dve", ti_[:, 0:T], ti_[:, 0:T], 1.0, xc[:, 0:T], ALU.add, ALU.mult,
                    reads=[ti_b, xc_b], writes=[ti_b])
                ACT(tr_[:, 0:T], tr_[:, 0:T], AF.Sqrt, bias=0.25, reads=[tr_b], writes=[tr_b])
                TT("dve", ti_[:, 0:T], tr_[:, 0:T], ti_[:, 0:T], ALU.mult, reads=[tr_b, ti_b], writes=[ti_b])
                pr.op("dve", lambda e: e.tensor_tensor_scan(tr_[:, 0:T], aa[:, 0:T], ti_[:, 0:T], hst[:, j:j + 1],
                                                            ALU.mult, ALU.add),
                      reads=[aa_b, ti_b, hst_b], writes=[tr_b])
                CP("dve", hst[:, j:j + 1], tr_[:, T - 1:T], reads=[tr_b], writes=[hst_b])
                TT("dve", qa[:, j, :], q[:, 0:T], tr_[:, 0:T], ALU.mult, reads=[q_b, tr_b], writes=[qa_b[j]])

            def ln_head():
                ACT(lnm[:], mbk[:], AF.Identity, reads=[mbk_b], writes=[lnm_b])
                ACT(lnr[:], mbk[:], AF.Square, reads=[mbk_b], writes=[lnr_b])
                STT("dve", lnr[:], lnr[:], -1.0, ebk[:], ALU.mult, ALU.add, reads=[lnr_b, ebk_b], writes=[lnr_b])
                TS1("dve", lnr[:], lnr[:], 0.0, ALU.max, reads=[lnr_b], writes=[lnr_b])
                ACT(lnr[:], lnr[:], AF.Ln, bias=EPS, reads=[lnr_b], writes=[lnr_b])
                ACT(lnr[:], lnr[:], AF.Exp, scale=-0.5, reads=[lnr_b], writes=[lnr_b])

            def ln_norm(j):
                cgc = big[:, T * j:T * (j + 1)]
                cgc_bufs = [act_b[2 * j], act_b[2 * j + 1]]
                nrm, nrm_b = ntmp()
                TT("dve", nrm[:, 0:T], cgc, lnm[:], ALU.subtract, reads=cgc_bufs + [lnm_b], writes=[nrm_b])
                TT("pool", nrm[:, 0:T], nrm[:, 0:T], lnr[:], ALU.mult, reads=[nrm_b, lnr_b], writes=[nrm_b])
                sj = big_bf[:, 8192 + T * j:8192 + T * (j + 1)]
                ACT(sj, nrm[:, 0:T], AF.Silu, scale=cvc("lng", j), bias=cvc("lnb", j),
                    reads=[nrm_b, const_b], writes=[act_b[16 + j]])

            LAG = 3
            pend = {}
            ln_sched = {KD + 1: (0, 1, 2), KD + 2: (3, 4, 5), KD + 3: (6, 7)}
            for i in range(KD + LAG + 2):
                if i < KD:
                    pr.tag = "B1"
                    b1(i)
                if 1 <= i <= KD:
                    pr.tag = "conv31"
                    conv(i - 1)
                    if i == KD:
                        pr.tag = "LN"
                        ln_head()
                if LAG <= i < KD + LAG:
                    pr.tag = "brA"
                    pend[i - LAG] = a_front(i - LAG)
                if LAG + 1 <= i < KD + LAG + 1:
                    pr.tag = "lru"
                    lru_tail(*pend.pop(i - LAG - 1))
                if i in ln_sched:
                    pr.tag = "LN"
                    for jj in ln_sched[i]:
                        ln_norm(jj)

            chk(7)
            qrhs = [qa[:, k, :] for k in range(KD)]
            srhs = [big_bf[:, 8192 + T * k:8192 + T * (k + 1)] for k in range(KD)]
            s_bufs = [act_b[16 + k] for k in range(KD)]
            for m in range(KD):
                bsa = proj_group(hrhs, hT_b)
                bsb = proj_group(hrhs, hT_b)
                ta, ta_b = ntmp()
                tb_, tb_b = ntmp()
                ACT(ta[:, 0:T], bsa[0][:], AF.Tanh, scale=0.5, bias=dvc("hbga", m), reads=[bsa[1], const_b], writes=[ta_b])
                ACT(tb_[:, 0:T], bsb[0][:], AF.Tanh, scale=0.5, bias=dvc("hbgb", m), reads=[bsb[1], const_b], writes=[tb_b])
                bA = proj_group(qrhs, qa_b)
                bB = proj_group(srhs, s_bufs)
                STT("dve", ta[:, 0:T], ta[:, 0:T], 1.0, bA[0][:], ALU.add, ALU.mult, reads=[ta_b, bA[1]], writes=[ta_b])
                STT("dve", tb_[:, 0:T], tb_[:, 0:T], 1.0, bB[0][:], ALU.add, ALU.mult, reads=[tb_b, bB[1]], writes=[tb_b])
                TT("pool", merged[:, m, :], ta[:, 0:T], tb_[:, 0:T], ALU.add, reads=[ta_b, tb_b], writes=[merged_b[m]])
            chk(8)
            mrhs = [merged[:, k, :] for k in range(KD)]
            for m in range(KD):
                bo = proj_group(mrhs, merged_b)
                STT("dve", xT[:, m, :], bo[0][:], 0.5, xT[:, m, :], ALU.mult, ALU.add,
                    reads=[bo[1], xT_b[m]], writes=[xT_b[m]])
                if m >= 2:
                    norm_acc(m - 2, m == 2, False)
            norm_acc(KD - 2, False, False)
            norm_acc(KD - 1, False, True)

            chk(9)
            rs, rs_b = norm_finish()
            for k in range(KD):
                STT("dve", hT[:, k, :], xT[:, k, :], cvc("g_ffn", k), rs[:, 0:T], ALU.mult, ALU.mult,
                    reads=[xT_b[k], rs_b, const_b], writes=[hT_b[k]])
            for j in range(KF):
                bg = proj_group(hrhs, hT_b)
                bv = proj_group(hrhs, hT_b)
                wk, wk_b = ntmp()
                CP("pool", wk[:, 0:2], gh[:, j, :], reads=[gh_b], writes=[wk_b])
                ACT(wk[:, 2:2 + T], bg[0][:], AF.Identity, reads=[bg[1]], writes=[wk_b])
                gc, gc_b = ntmp()
                o2 = _CV["fw"] + 24 * 2 + j
                ACT(gc[:, 0:T], bg[0][:], AF.Identity, scale=cv[:, o2:o2 + 1], bias=cvc("fb", j),
                    reads=[bg[1], const_b], writes=[gc_b])
                CP("pool", gh[:, j, :], wk[:, T:T + 2], reads=[wk_b], writes=[gh_b])
                for k in (1, 0):
                    o = _CV["fw"] + 24 * k + j
                    STT("dve", gc[:, 0:T], wk[:, k:k + T], cv[:, o:o + 1], gc[:, 0:T], ALU.mult, ALU.add,
                        reads=[wk_b, gc_b, const_b], writes=[gc_b])
                ACT(gc[:, 0:T], gc[:, 0:T], AF.Gelu_apprx_tanh, reads=[gc_b], writes=[gc_b])
                TT("dve", big_bf[:, T * j:T * (j + 1)], gc[:, 0:T], bv[0][:], ALU.mult,
                   reads=[gc_b, bv[1]], writes=[act_b[j]])
            chk(11)
            for m in range(KD):
                bk = nbank()
                for th3 in range(3):
                    arhs = [big_bf[:, T * (8 * th3 + k):T * (8 * th3 + k + 1)] for k in range(KD)]
                    proj_group(arhs, act_b[8 * th3:8 * th3 + 8], bank=bk, first=(th3 == 0), last=(th3 == 2))
                TT("dve", xT[:, m, :], bk[0][:], xT[:, m, :], ALU.add, reads=[bk[1], xT_b[m]], writes=[xT_b[m]])
                if m >= 2:
                    norm_acc(m - 2, m == 2, False)
            norm_acc(KD - 2, False, False)
            norm_acc(KD - 1, False, True)

            chk(12)
            rs, rs_b = norm_finish()
            for k in range(KD):
                STT("dve", xT[:, k, :], xT[:, k, :], cvc("g_fin", k), rs[:, 0:T], ALU.mult, ALU.mult,
                    reads=[xT_b[k], rs_b, const_b], writes=[xT_b[k]])
            for tt in range(4):
                yo, yo_b = yout[tt % 2], yout_b[tt % 2]
                for half in range(2):
                    bk, bk_b = nbank()
                    for kk in range(4):
                        k = 4 * half + kk
                        TR(bk[:, P * kk:P * (kk + 1)], xT[:, k, P * tt:P * (tt + 1)], reads=[xT_b[k], const_b],
                           writes=[bk_b], sig=(kk == 3))
                    if half == 0:
                        ACT(yo[:, 0:T], bk[:], AF.Identity, reads=[bk_b], writes=[yo_b])
                    else:
                        CP("dve", yo[:, T:2 * T], bk[:], reads=[bk_b], writes=[yo_b])
                pr.dma("act", ("yout", tt % 2), y_d[t0 + P * tt:t0 + P * (tt + 1), :], yo[:], reads=[yo_b])
          except _Stop:
            break

        pr.wait_all("act", [("yout", 0), ("yout", 1)])
        pr.wait_all("sp", [("yout", 0), ("yout", 1)])

        def replay(name, eng):
            for waits, emit, inc, dsem in pr.ops[name]:
                for k, v in waits:
                    eng.wait_ge(sems[k], v)
                if emit is None:
                    continue
                ins = emit(eng)
                if dsem is not None:
                    ins.then_inc(sems[dsem], 16)
                elif inc:
                    ins.then_inc(sems[name], 1)

        with nc.Block() as block:
            @block.sync
            def _(e):
                replay("sp", e)

            @block.tensor
            def _(e):
                replay("pe", e)

            @block.scalar
            def _(e):
                replay("act", e)

            @block.vector
            def _(e):
                replay("dve", e)

            @block.gpsimd
            def _(e):
                replay("pool", e)
    nc._prog = pr
    return nc


def _fm(v):
    v = np.asarray(v, np.float32)
    return np.ascontiguousarray(v.reshape(-1, P).T)


def _slab(W, r0, c0):
    return W[r0:r0 + 1024, c0:c0 + P].reshape(KD, P, P).transpose(1, 0, 2)


def _weight_stream(w_in, w_a, w_b, w_out, w_up, w_down):
    slabs = []
    LAG = 3
    for i in range(KD + LAG):
        if i < KD:
            slabs.append(_slab(w_in, 0, 3072 + P * i))
            slabs.append(_slab(w_in, 0, 2048 + P * i))
        if LAG <= i < KD + LAG:
            j = i - LAG
            slabs.append(_slab(w_in, 0, P * j))
            slabs.append(_slab(w_in, 0, 1024 + P * j))
    for m in range(KD):
        slabs.append(_slab(w_in, 0, 4096 + P * m))
        slabs.append(_slab(w_in, 0, 5120 + P * m))
        slabs.append(_slab(w_a, 0, P * m))
        slabs.append(_slab(w_b, 0, P * m))
    for m in range(KD):
        slabs.append(_slab(w_out, 0, P * m))
    for j in range(KF):
        slabs.append(_slab(w_up, 0, P * j))
        slabs.append(_slab(w_up, 0, DFF + P * j))
    for m in range(KD):
        for th3 in range(3):
            slabs.append(_slab(w_down, 1024 * th3, P * m))
    assert len(slabs) == 4 * NGRP
    st = np.stack(slabs).reshape(NGRP, 4, P, KD * P).transpose(0, 2, 1, 3)
    return np.ascontiguousarray(st.reshape(NGRP, P, 4096), dtype=np.float32)


def _cvec(inp):
    cvv = np.zeros((P, NV), np.float32)

    def put(name, arr):
        arr = np.asarray(arr, np.float32)
        cvv[:, _CV[name]:_CV[name] + arr.shape[1]] = arr
    put("g_mix", _fm(inp["g_mix"][0]))
    put("lcw", np.concatenate([_fm(inp["lru_conv_w"][0, k]) for k in range(4)], axis=1))
    put("lcb", _fm(inp["lru_conv_b"][0]))
    put("lba", _fm(inp["lru_ba"][0]))
    put("lbx", _fm(inp["lru_bx"][0]))
    put("lam", _fm(inp["lru_lambda"][0]))
    put("cw", np.concatenate([_fm(inp["conf_dw_w"][0, k]) for k in range(31)], axis=1))
    put("cb", _fm(inp["conf_dw_b"][0]))
    put("lng", _fm(inp["conf_ln_g"][0]))
    put("lnb", _fm(inp["conf_ln_b"][0]))
    put("bga", _fm(inp["b_gate"][0, :D]))
    put("bgb", _fm(inp["b_gate"][0, D:]))
    put("g_ffn", _fm(inp["g_ffn"][0]))
    put("fw", np.concatenate([_fm(inp["ffn_dw_w"][0, k]) for k in range(3)], axis=1))
    put("fb", _fm(inp["ffn_dw_b"][0]))
    put("g_fin", _fm(inp["g_final"]))
    return cvv


_NC_CACHE = {}


def kernel(**inputs):
    inp = {k: np.asarray(v) for k, v in inputs.items()}
    if "nc" not in _NC_CACHE:
        _NC_CACHE["nc"] = build_program()
    nc = _NC_CACHE["nc"]
    wst = _weight_stream(inp["w_in"][0], inp["w_lru_out"][0], inp["w_conf_out"][0], inp["w_out"][0],
                         inp["w_up"][0], inp["w_down"][0])
    cvv = _cvec(inp)
    wa = np.ascontiguousarray(inp["lru_wa"][0], dtype=np.float32)
    wx = np.ascontiguousarray(inp["lru_wx"][0], dtype=np.float32)
    x = np.asarray(inp["x"], np.float32)
    in_maps = [{"x": np.ascontiguousarray(x[b]), "wst": wst, "cvec": cvv, "lru_wa": wa, "lru_wx": wx}
               for b in range(8)]
    res = run_bass_kernel_spmd(nc, in_maps, core_ids=list(range(8)))
    return np.stack([np.asarray(res.results[b]["y"], dtype=np.float32).reshape(S, D) for b in range(8)])
```

```python
import numpy as np
from contextlib import ExitStack
import concourse.bass as bass
import concourse.mybir as mybir
from concourse.bass_utils import run_bass_kernel_spmd

F32 = mybir.dt.float32
BF16 = mybir.dt.bfloat16
AF = mybir.ActivationFunctionType
ALU = mybir.AluOpType

P = 128
S = 4096
D = 1024
T = 512
NCH = S // T
KD = 8
DFF = 3072
KF = 24
EPS = 1e-6
K_GELU = 0.7978845608028654
C_GELU = 0.21145921
NGRP = 36
NSLOT = 4
N_TMP = 17
N_TB = 4

_CV = {}
def _mk_cv():
    o = 0
    for name, n in (("g_mix", 8), ("lcw", 32), ("lcb", 8), ("lba", 8), ("lbx", 8), ("lam", 8),
                    ("cw", 248), ("cb", 8), ("lng", 8), ("lnb", 8), ("bga", 8), ("bgb", 8),
                    ("g_ffn", 8), ("fw", 72), ("fb", 24), ("g_fin", 8)):
        _CV[name] = o
        o += n
    return o
NV = _mk_cv()
_DV = {n: 8 * i for i, n in enumerate(("hba", "hbx", "hbga", "hbgb", "c1", "hc1", "hlng", "hlnb", "e", "l"))}
NDV = 80


class Buf:
    __slots__ = ("name", "w", "r")

    def __init__(self, name):
        self.name = name
        self.w = None
        self.r = {}


class Prog:
    ENGS = ("pe", "act", "dve", "pool", "sp")

    def __init__(self):
        self.ops = {e: [] for e in self.ENGS}
        self.cnt = {e: 0 for e in self.ENGS}
        self.waited = {e: {} for e in self.ENGS}
        self.dcnt = {}
        self.tag = ""
        self.tags = {e: [] for e in self.ENGS}

    def _deps(self, eng, reads, writes):
        need = {}

        def add(k, v):
            if k == "pe" and eng == "pe":
                return
            if need.get(k, 0) < v:
                need[k] = v
        for b in reads:
            if b.w is not None:
                add(*b.w)
        for b in writes:
            if b.w is not None:
                add(*b.w)
            for k, v in b.r.items():
                add(k, v)
        wd = self.waited[eng]
        waits = []
        for k, v in need.items():
            if wd.get(k, 0) < v:
                wd[k] = v
                waits.append((k, v))
        return waits

    def _mark(self, tok, reads, writes):
        k, v = tok
        for b in reads:
            if b.r.get(k, 0) < v:
                b.r[k] = v
        for b in writes:
            b.w = tok
            b.r = {}

    def op(self, eng, emit, reads=(), writes=(), inc=True):
        waits = self._deps(eng, reads, writes)
        if inc:
            self.cnt[eng] += 1
            tok = (eng, self.cnt[eng])
        else:
            tok = (eng, self.cnt[eng] + 1)
        self._mark(tok, reads, writes)
        self.ops[eng].append((waits, emit, inc, None))
        self.tags[eng].append(self.tag)

    def dma(self, eng, semkey, out, in_, reads=(), writes=()):
        waits = self._deps(eng, reads, writes)
        self.dcnt[semkey] = self.dcnt.get(semkey, 0) + 16
        tok = (semkey, self.dcnt[semkey])
        self._mark(tok, reads, writes)
        self.ops[eng].append((waits, (lambda e, o=out, i=in_: e.dma_start(out=o, in_=i)), False, semkey))

    def wait_all(self, eng, keys):
        waits = []
        for k in keys:
            v = self.dcnt.get(k, 0) if not isinstance(k, str) else self.cnt[k]
            if v and self.waited[eng].get(k, 0) < v:
                self.waited[eng][k] = v
                waits.append((k, v))
        self.ops[eng].append((waits, None, False, None))


class _Stop(Exception):
    pass


def build_program(nch=NCH, stop=99):
    nc = bass.Bass("TRN2", target_bir_lowering=False)
    x_d = nc.dram_tensor("x", [S, D], F32, kind="ExternalInput").ap()
    wst_d = nc.dram_tensor("wst", [NGRP, P, 4096], F32, kind="ExternalInput").ap()
    cvec_d = nc.dram_tensor("cvec", [P, NV], F32, kind="ExternalInput").ap()
    wa_d = nc.dram_tensor("lru_wa", [16, 64, 64], F32, kind="ExternalInput").ap()
    wx_d = nc.dram_tensor("lru_wx", [16, 64, 64], F32, kind="ExternalInput").ap()
    y_d = nc.dram_tensor("y", [S, D], F32, kind="ExternalOutput").ap()
    wbf_d = nc.dram_tensor("wbf", [NGRP, P, 4096], BF16, kind="Internal").ap()
    dgs_d = nc.dram_tensor("dgs", [KD, P, 31 * P], BF16, kind="Internal").ap()

    pr = Prog()
    with ExitStack() as es:
        def sb(name, shape, dt):
            return es.enter_context(nc.sbuf_tensor(name, shape, dt))

        xT = sb("xT", [P, KD, T], F32)
        xT_b = [Buf(f"xT{k}") for k in range(KD)]
        xin = [sb(f"xin{i}", [P, D], F32) for i in range(2)]
        xin_b = [Buf(f"xin{i}") for i in range(2)]
        yout = [sb(f"yout{i}", [P, D], F32) for i in range(2)]
        yout_b = [Buf(f"yout{i}") for i in range(2)]
        hT = sb("hT", [P, KD, T], BF16)
        hT_b = [Buf(f"hT{k}") for k in range(KD)]
        tmps = [sb(f"tmp{i}", [P, 520], F32) for i in range(N_TMP)]
        tmps_b = [Buf(f"tmp{i}") for i in range(N_TMP)]
        tbs = [sb(f"tb{i}", [P, T], BF16) for i in range(N_TB)]
        tbs_b = [Buf(f"tb{i}") for i in range(N_TB)]
        qa = sb("qa", [P, KD, T], BF16)
        qa_b = [Buf(f"qa{k}") for k in range(KD)]
        merged = sb("merged", [P, KD, T], BF16)
        merged_b = [Buf(f"mg{k}") for k in range(KD)]
        big = sb("big", [P, 6144], F32)
        big_bf = big.bitcast(BF16)
        act_b = [Buf(f"act{j}") for j in range(KF)]
        cgbuf = sb("cgbuf", [P, KD, 544], BF16)
        cg_b = [Buf(f"cg{j}") for j in range(KD)]
        dg = [sb(f"dg{i}", [P, 31, P], BF16) for i in range(2)]
        dg_b = [Buf(f"dg{i}") for i in range(2)]
        ring = [sb(f"ring{i}", [P, 4, KD, P], BF16) for i in range(NSLOT)]
        ring_b = [Buf(f"ring{i}") for i in range(NSLOT)]
        cv = sb("cv", [P, NV], F32)
        dv = sb("dv", [P, NDV], F32)
        const_b = Buf("const")
        ident_f = sb("ident_f", [P, P], F32)
        ones_f = sb("ones_f", [P, P], F32)
        ident_b = sb("ident_b", [P, P], BF16)
        bd_f = sb("bd_f", [P, KD, P], F32)
        bd_f_b = Buf("bd_f")
        bd_b = sb("bd_b", [P, 2, KD, P], BF16)
        xah = sb("xah", [P, KD, 3], F32)
        xah_b = Buf("xah")
        gh = sb("gh", [P, KF, 2], F32)
        gh_b = Buf("gh")
        hst = sb("hst", [P, KD], F32)
        hst_b = Buf("hst")
        tab = sb("tab", [P, 2 * KD, T], BF16)
        tab_b = [Buf(f"tab{i}") for i in range(2 * KD)]
        lnm = sb("lnm", [P, T], F32)
        lnm_b = Buf("lnm")
        lnr = sb("lnr", [P, T], F32)
        lnr_b = Buf("lnr")
        banks = [es.enter_context(nc.psum_tensor(f"bank{i}", [P, T], F32)) for i in range(8)]
        banks_b = [Buf(f"bank{i}") for i in range(8)]
        wbf_b = [Buf(f"wbf{q}") for q in range(NGRP)]
        dgs_b = [Buf(f"dgs{j}") for j in range(KD)]

        sems = {}
        for e in ("pe", "act", "dve", "pool"):
            sems[e] = es.enter_context(nc.semaphore(f"s_{e}"))
        dkeys = ([("ring", i) for i in range(NSLOT)] + [("xin", i) for i in range(2)]
                 + [("yout", i) for i in range(2)] + [("pre", i) for i in range(4)] + [("cst", 0), ("cst", 1)]
                 + [("dg", i) for i in range(2)] + [("dgs", i) for i in range(2)])
        for k in dkeys:
            sems[k] = es.enter_context(nc.semaphore(f"d_{k[0]}{k[1]}"))

        def cvc(name, i):
            o = _CV[name] + i
            return cv[:, o:o + 1]

        def dvc(name, i):
            o = _DV[name] + i
            return dv[:, o:o + 1]

        def ACT(out, in_, func, scale=1.0, bias=0.0, reads=(), writes=()):
            pr.op("act", lambda e: e.activation(out, in_, func, bias=bias, scale=scale), reads, writes)

        def TS(eng, out, in0, s1, s2, op0, op1, reads=(), writes=()):
            pr.op(eng, lambda e: e.tensor_scalar(out, in0, s1, s2, op0, op1), reads, writes)

        def TS1(eng, out, in0, s1, op0, reads=(), writes=()):
            pr.op(eng, lambda e: e.tensor_single_scalar(out, in0, s1, op0), reads, writes)

        def STT(eng, out, in0, scalar, in1, op0, op1, reads=(), writes=()):
            pr.op(eng, lambda e: e.scalar_tensor_tensor(out, in0, scalar, in1, op0, op1), reads, writes)

        def TT(eng, out, in0, in1, op, reads=(), writes=()):
            pr.op(eng, lambda e: e.tensor_tensor(out, in0, in1, op), reads, writes)

        def CP(eng, out, in_, reads=(), writes=()):
            pr.op(eng, lambda e: e.tensor_copy(out, in_), reads, writes)

        def MM(out, lhsT, rhs, start, stop, reads, writes, sig):
            pr.op("pe", lambda e: e.matmul(out, lhsT, rhs, start=start, stop=stop), reads, writes, inc=sig)

        def TR(out, in_, reads, writes, sig):
            pr.op("pe", lambda e: e.transpose(out, in_, ident_f[:]), reads, writes, inc=sig)

        st = {"bank": 0, "tmp": 0, "tb": 0, "grp": 0}

        def nbank():
            i = st["bank"]
            st["bank"] = (i + 1) % 6
            return banks[i], banks_b[i]

        def ntmp():
            i = st["tmp"]
            st["tmp"] = (i + 1) % N_TMP
            return tmps[i], tmps_b[i]

        def ntb():
            i = st["tb"]
            st["tb"] = (i + 1) % N_TB
            return tbs[i], tbs_b[i]

        wstate = {"slab": 0, "loaded": 0}

        def load_groups_upto(gidx):
            while wstate["loaded"] <= gidx:
                g = wstate["loaded"]
                if g < NGRP:
                    prepass_upto(g + 4)
                q = g % NGRP
                slot = g % NSLOT
                pr.dma("sp", ("ring", slot), ring[slot][:].rearrange("p s k n -> p (s k n)"), wbf_d[q],
                       reads=[wbf_b[q]], writes=[ring_b[slot]])
                wstate["loaded"] += 1

        def next_slab():
            s_ = wstate["slab"]
            wstate["slab"] += 1
            g = s_ // 4
            load_groups_upto(min(g + NSLOT - 1, nch * NGRP - 1))
            slot = g % NSLOT
            return ring[slot][:, s_ % 4], ring_b[slot]

        def proj_group(rhs_list, rhs_bufs, bank=None, first=True, last=True):
            slab, slab_b = next_slab()
            if bank is None:
                bank = nbank()
            bk, bk_b = bank
            n = len(rhs_list)
            for k in range(n):
                MM(bk[:], slab[:, k, :], rhs_list[k], start=(first and k == 0), stop=(last and k == n - 1),
                   reads=[slab_b] + list(rhs_bufs), writes=[bk_b], sig=(k == n - 1))
            return bank

        pr.dma("sp", ("cst", 0), cv[:], cvec_d, writes=[const_b])
        pr.op("pool", lambda e: e.memset(ones_f[:], 1.0 / D), writes=[const_b])
        pr.op("pool", lambda e: e.memset(ident_f[:], 1.0), writes=[const_b])
        pr.op("pool", lambda e: e.affine_select(out=ident_f[:], in_=ident_f[:], pattern=[[-1, P]],
                                                compare_op=ALU.is_equal, fill=0.0, base=0,
                                                channel_multiplier=1), reads=[const_b], writes=[const_b])
        pr.op("pool", lambda e: e.memset(xah[:], 0.0), writes=[xah_b])
        pr.op("pool", lambda e: e.memset(gh[:], 0.0), writes=[gh_b])
        pr.op("pool", lambda e: e.memset(hst[:], 0.0), writes=[hst_b])
        pr.op("pool", lambda e: e.memset(cgbuf[:], 0.0), writes=cg_b)
        CP("dve", ident_b[:], ident_f[:], reads=[const_b], writes=[const_b])
        for gi, w_d in enumerate((wa_d, wx_d)):
            pr.op("pool", lambda e: e.memset(bd_f[:], 0.0), writes=[bd_f_b])
            wv = w_d.rearrange("(j two) d e -> two d j e", two=2)
            pr.dma("sp", ("cst", 1), bd_f[0:64, :, 0:64], wv[0], writes=[bd_f_b])
            pr.dma("sp", ("cst", 1), bd_f[64:128, :, 64:128], wv[1], writes=[bd_f_b])
            CP("dve", bd_b[:, gi], bd_f[:], reads=[bd_f_b], writes=[const_b])
        pstate = {"n": 0}

        def prepass_upto(q):
            while pstate["n"] <= min(q, NGRP - 1):
                qq = pstate["n"]
                pr.dma("pool", ("pre", qq % 4), wbf_d[qq], wst_d[qq], reads=([wbf_b[qq - 4]] if qq >= 4 else []),
                       writes=[wbf_b[qq]])
                pstate["n"] += 1

        for nm, src in (("hba", "lba"), ("hbx", "lbx"), ("hbga", "bga"), ("hbgb", "bgb"),
                        ("hlng", "lng"), ("hlnb", "lnb")):
            TS1("dve", dv[:, _DV[nm]:_DV[nm] + 8], cv[:, _CV[src]:_CV[src] + 8], 0.5, ALU.mult,
                reads=[const_b], writes=[const_b])
        ACT(dv[:, _DV["e"]:_DV["e"] + 8], cv[:, _CV["lam"]:_CV["lam"] + 8], AF.Exp, scale=-1.0,
            reads=[const_b], writes=[const_b])
        ACT(dv[:, _DV["l"]:_DV["l"] + 8], dv[:, _DV["e"]:_DV["e"] + 8], AF.Ln, scale=1.0, bias=1.0,
            reads=[const_b], writes=[const_b])
        TS1("dve", dv[:, _DV["c1"]:_DV["c1"] + 8], dv[:, _DV["l"]:_DV["l"] + 8], -8.0, ALU.mult,
            reads=[const_b], writes=[const_b])
        TS1("dve", dv[:, _DV["hc1"]:_DV["hc1"] + 8], dv[:, _DV["l"]:_DV["l"] + 8], -4.0, ALU.mult,
            reads=[const_b], writes=[const_b])

        def norm_acc(k, first, last):
            sq, sq_b = ntmp()
            ACT(sq[:, 0:T], xT[:, k, :], AF.Square, reads=[xT_b[k]], writes=[sq_b])
            MM(banks[6][:], ones_f[:], sq[:, 0:T], start=first, stop=last,
               reads=[sq_b, const_b], writes=[banks_b[6]], sig=True)

        def norm_finish():
            rs, rs_b = ntmp()
            ACT(rs[:, 0:T], banks[6][:], AF.Ln, bias=EPS, reads=[banks_b[6]], writes=[rs_b])
            ACT(rs[:, 0:T], rs[:, 0:T], AF.Exp, scale=-0.5, reads=[rs_b], writes=[rs_b])
            return rs, rs_b

        names = {1: "S0", 2: "norm1", 3: "mix", 6: "x", 4: "conv31", 5: "LN", 6: "brA", 7: "merge", 8: "Wout", 9: "ffn_norm_up",
                 11: "down", 12: "final"}

        def chk(n):
            if n > stop:
                raise _Stop()
            pr.tag = names[n]

        for c in range(nch):
          try:
            t0 = c * T
            chk(1)
            for tt in range(4):
                xi, xi_b = xin[tt % 2], xin_b[tt % 2]
                pr.dma("sp", ("xin", tt % 2), xi[:], x_d[t0 + P * tt:t0 + P * (tt + 1), :], writes=[xi_b])
                for half in range(2):
                    bk, bk_b = nbank()
                    for kk in range(4):
                        k = 4 * half + kk
                        TR(bk[:, P * kk:P * (kk + 1)], xi[:, P * k:P * (k + 1)], reads=[xi_b, const_b],
                           writes=[bk_b], sig=(kk == 3))
                    ACT(xT[:, 4 * half:4 * half + 4, P * tt:P * (tt + 1)],
                        bk[:].rearrange("p (a b) -> p a b", a=4), AF.Identity,
                        reads=[bk_b], writes=xT_b[4 * half:4 * half + 4])
            chk(2)
            for k in range(KD):
                norm_acc(k, k == 0, k == KD - 1)
            rs, rs_b = norm_finish()
            for k in range(KD):
                STT("dve", hT[:, k, :], xT[:, k, :], cvc("g_mix", k), rs[:, 0:T], ALU.mult, ALU.mult,
                    reads=[xT_b[k], rs_b, const_b], writes=[hT_b[k]])
            hrhs = [hT[:, k, :] for k in range(KD)]

            chk(3)
            mbk, mbk_b = banks[6], banks_b[6]
            ebk, ebk_b = banks[7], banks_b[7]

            def b1(j):
                bcb = proj_group(hrhs, hT_b)
                bca = proj_group(hrhs, hT_b)
                tg, tg_b = ntmp()
                ACT(tg[:, 0:T], bcb[0][:], AF.Tanh, scale=0.5, reads=[bcb[1]], writes=[tg_b])
                STT("dve", cgbuf[:, j, 30:542], tg[:, 0:T], 1.0, bca[0][:], ALU.add, ALU.mult,
                    reads=[tg_b, bca[1]], writes=[cg_b[j]])

            def a_front(j):
                bxa = proj_group(hrhs, hT_b)
                bga = proj_group(hrhs, hT_b)
                wk, wk_b = ntmp()
                CP("pool", wk[:, 0:3], xah[:, j, :], reads=[xah_b], writes=[wk_b])
                ACT(wk[:, 3:3 + T], bxa[0][:], AF.Identity, reads=[bxa[1]], writes=[wk_b])
                xc, xc_b = ntmp()
                ACT(xc[:, 0:T], bxa[0][:], AF.Identity, scale=cv[:, _CV["lcw"] + 24 + j:_CV["lcw"] + 25 + j],
                    bias=cvc("lcb", j), reads=[bxa[1], const_b], writes=[xc_b])
                CP("pool", xah[:, j, :], wk[:, T:T + 3], reads=[wk_b], writes=[xah_b])
                q, q_b = ntmp()
                ACT(q[:, 0:T], bga[0][:], AF.Gelu_apprx_tanh, reads=[bga[1]], writes=[q_b])
                for k in (2, 1, 0):
                    o = _CV["lcw"] + 8 * k + j
                    STT("dve", xc[:, 0:T], wk[:, k:k + T], cv[:, o:o + 1], xc[:, 0:T], ALU.mult, ALU.add,
                        reads=[wk_b, xc_b, const_b], writes=[xc_b])
                xcb, xcb_b = ntb()
                ACT(xcb[:], xc[:, 0:T], AF.Identity, reads=[xc_b], writes=[xcb_b])
                return (j, xc, xc_b, xcb, xcb_b, q, q_b)

            def dg_load(g):
                pr.dma("pool", ("dg", g % 2), dg[g % 2][:].rearrange("p k n -> p (k n)"), dgs_d[g % KD],
                       reads=[dgs_b[g % KD]], writes=[dg_b[g % 2]])

            def dg_build(j):
                for k in range(31):
                    o = _CV["cw"] + 8 * k + j
                    pr.op("dve", (lambda e, d=dg[j % 2], k=k, o=o: e.tensor_scalar_mul(d[:, k, :], ident_b[:], cv[:, o:o + 1])),
                          reads=[const_b], writes=[dg_b[j % 2]])
                pr.dma("pool", ("dgs", j % 2), dgs_d[j], dg[j % 2][:].rearrange("p k n -> p (k n)"),
                       reads=[dg_b[j % 2]], writes=[dgs_b[j]])

            def conv(j):
                dgi = (c * KD + j) % 2
                if c == 0 and j == 0:
                    dg_build(0)
                bk, bk_b = nbank()
                for k in range(31):
                    MM(bk[:], dg[dgi][:, k, :], cgbuf[:, j, k:k + T], start=(k == 0), stop=(k == 30),
                       reads=[dg_b[dgi], cg_b[j]], writes=[bk_b], sig=(k == 30))
                if c == 0 and j + 1 < KD:
                    dg_build(j + 1)
                elif c * KD + j + 1 < nch * KD:
                    dg_load(c * KD + j + 1)
                CP("pool", cgbuf[:, j, 0:30], cgbuf[:, j, T:T + 30], reads=[cg_b[j]], writes=[cg_b[j]])
                cgc = big[:, T * j:T * (j + 1)]
                cgc_bufs = [act_b[2 * j], act_b[2 * j + 1]]
                ACT(cgc, bk[:], AF.Identity, scale=0.5, bias=cvc("cb", j), reads=[bk_b, const_b], writes=cgc_bufs)
                sq, sq_b = ntmp()
                TT("pool", sq[:, 0:T], cgc, cgc, ALU.mult, reads=cgc_bufs, writes=[sq_b])
                MM(mbk[:], ones_f[:], cgc, start=(j == 0), stop=(j == KD - 1), reads=cgc_bufs + [const_b],
                   writes=[mbk_b], sig=True)
                MM(ebk[:], ones_f[:], sq[:, 0:T], start=(j == 0), stop=(j == KD - 1), reads=[sq_b, const_b],
                   writes=[ebk_b], sig=True)

            def lru_tail(j, xc, xc_b, xcb, xcb_b, q, q_b):
                br = nbank()
                MM(br[0][:], bd_b[:, 0, j, :], xcb[:], start=True, stop=True, reads=[xcb_b, const_b],
                   writes=[br[1]], sig=True)
                bi = nbank()
                MM(bi[0][:], bd_b[:, 1, j, :], xcb[:], start=True, stop=True, reads=[xcb_b, const_b],
                   writes=[bi[1]], sig=True)
                tr_, tr_b = ntmp()
                ti_, ti_b = ntmp()
                aa, aa_b = ntmp()
                ACT(tr_[:, 0:T], br[0][:], AF.Tanh, scale=0.5, bias=dvc("hba", j), reads=[br[1], const_b], writes=[tr_b])
                ACT(ti_[:, 0:T], bi[0][:], AF.Tanh, scale=0.5, bias=dvc("hbx", j), reads=[bi[1], const_b], writes=[ti_b])
                ACT(aa[:, 0:T], tr_[:, 0:T], AF.Exp, scale=dvc("hc1", j), bias=dvc("hc1", j),
                    reads=[tr_b, const_b], writes=[aa_b])
                ACT(tr_[:, 0:T], tr_[:, 0:T], AF.Exp, scale=dvc("c1", j), bias=dvc("c1", j),
                    reads=[tr_b, const_b], writes=[tr_b])
                TS("dve", tr_[:, 0:T], tr_[:, 0:T], 1.0, -0.25, ALU.min, ALU.mult, reads=[tr_b], writes=[tr_b])
                STT("dve", ti_[:, 0:T], ti_[:, 0:T], 1.0, xc[:, 0:T], ALU.add, ALU.mult,
                    reads=[ti_b, xc_b], writes=[ti_b])
                ACT(tr_[:, 0:T], tr_[:, 0:T], AF.Sqrt, bias=0.25, reads=[tr_b], writes=[tr_b])
                TT("dve", ti_[:, 0:T], tr_[:, 0:T], ti_[:, 0:T], ALU.mult, reads=[tr_b, ti_b], writes=[ti_b])
                pr.op("dve", lambda e: e.tensor_tensor_scan(tr_[:, 0:T], aa[:, 0:T], ti_[:, 0:T], hst[:, j:j + 1],
                                                            ALU.mult, ALU.add),
                      reads=[aa_b, ti_b, hst_b], writes=[tr_b])
                CP("dve", hst[:, j:j + 1], tr_[:, T - 1:T], reads=[tr_b], writes=[hst_b])
                TT("dve", qa[:, j, :], q[:, 0:T], tr_[:, 0:T], ALU.mult, reads=[q_b, tr_b], writes=[qa_b[j]])

            def ln_head():
                ACT(lnm[:], mbk[:], AF.Identity, reads=[mbk_b], writes=[lnm_b])
                ACT(lnr[:], mbk[:], AF.Square, reads=[mbk_b], writes=[lnr_b])
                STT("dve", lnr[:], lnr[:], -1.0, ebk[:], ALU.mult, ALU.add, reads=[lnr_b, ebk_b], writes=[lnr_b])
                TS1("dve", lnr[:], lnr[:], 0.0, ALU.max, reads=[lnr_b], writes=[lnr_b])
                ACT(lnr[:], lnr[:], AF.Ln, bias=EPS, reads=[lnr_b], writes=[lnr_b])
                ACT(lnr[:], lnr[:], AF.Exp, scale=-0.5, reads=[lnr_b], writes=[lnr_b])

            def ln_norm_a(j):
                cgc = big[:, T * j:T * (j + 1)]
                cgc_bufs = [act_b[2 * j], act_b[2 * j + 1]]
                nrm, nrm_b = ntmp()
                TT("dve", nrm[:, 0:T], cgc, lnm[:], ALU.subtract, reads=cgc_bufs + [lnm_b], writes=[nrm_b])
                TT("pool", nrm[:, 0:T], nrm[:, 0:T], lnr[:], ALU.mult, reads=[nrm_b, lnr_b], writes=[nrm_b])
                return nrm, nrm_b

            def ln_norm_b(j, nrm, nrm_b):
                sj = big_bf[:, 8192 + T * j:8192 + T * (j + 1)]
                ACT(sj, nrm[:, 0:T], AF.Silu, scale=cvc("lng", j), bias=cvc("lnb", j),
                    reads=[nrm_b, const_b], writes=[act_b[16 + j]])

            pend = {}
            for i in range(KD + 2):
                if i < KD:
                    pr.tag = "brA"
                    pend[i] = a_front(i)
                if 1 <= i <= KD:
                    pr.tag = "lru"
                    lru_tail(*pend.pop(i - 1))
                    pr.tag = "B1"
                    b1(i - 1)
                if 2 <= i <= KD + 1:
                    pr.tag = "conv31"
                    conv(i - 2)
            pr.tag = "LN"
            ln_head()

            chk(7)
            npend = None
            for m in range(KD):
                pr.tag = "LN"
                nn = ln_norm_a(m)
                pr.tag = "gates"
                bsa = proj_group(hrhs, hT_b)
                bsb = proj_group(hrhs, hT_b)
                ACT(tab[:, m, :], bsa[0][:], AF.Tanh, scale=0.5, bias=dvc("hbga", m), reads=[bsa[1], const_b],
                    writes=[tab_b[m]])
                ACT(tab[:, KD + m, :], bsb[0][:], AF.Tanh, scale=0.5, bias=dvc("hbgb", m), reads=[bsb[1], const_b],
                    writes=[tab_b[KD + m]])
                pr.tag = "LN"
                if npend is not None:
                    ln_norm_b(*npend)
                npend = (m,) + nn
            ln_norm_b(*npend)
            pr.tag = "merge"
            qrhs = [qa[:, k, :] for k in range(KD)]
            srhs = [big_bf[:, 8192 + T * k:8192 + T * (k + 1)] for k in range(KD)]
            s_bufs = [act_b[16 + k] for k in range(KD)]
            for m in range(KD):
                bA = proj_group(qrhs, qa_b)
                bB = proj_group(srhs, s_bufs)
                ta, ta_b = ntmp()
                tb_, tb_b = ntmp()
                STT("dve", ta[:, 0:T], tab[:, m, :], 1.0, bA[0][:], ALU.add, ALU.mult, reads=[tab_b[m], bA[1]], writes=[ta_b])
                STT("dve", tb_[:, 0:T], tab[:, KD + m, :], 1.0, bB[0][:], ALU.add, ALU.mult, reads=[tab_b[KD + m], bB[1]],
                    writes=[tb_b])
                TT("pool", merged[:, m, :], ta[:, 0:T], tb_[:, 0:T], ALU.add, reads=[ta_b, tb_b], writes=[merged_b[m]])
            chk(8)
            mrhs = [merged[:, k, :] for k in range(KD)]
            for m in range(KD):
                bo = proj_group(mrhs, merged_b)
                STT("dve", xT[:, m, :], bo[0][:], 0.5, xT[:, m, :], ALU.mult, ALU.add,
                    reads=[bo[1], xT_b[m]], writes=[xT_b[m]])
                if m >= 2:
                    norm_acc(m - 2, m == 2, False)
            norm_acc(KD - 2, False, False)
            norm_acc(KD - 1, False, True)

            chk(9)
            rs, rs_b = norm_finish()
            for k in range(KD):
                STT("dve", hT[:, k, :], xT[:, k, :], cvc("g_ffn", k), rs[:, 0:T], ALU.mult, ALU.mult,
                    reads=[xT_b[k], rs_b, const_b], writes=[hT_b[k]])
            for j in range(KF):
                bg = proj_group(hrhs, hT_b)
                bv = proj_group(hrhs, hT_b)
                wk, wk_b = ntmp()
                CP("pool", wk[:, 0:2], gh[:, j, :], reads=[gh_b], writes=[wk_b])
                ACT(wk[:, 2:2 + T], bg[0][:], AF.Identity, reads=[bg[1]], writes=[wk_b])
                gc, gc_b = ntmp()
                o2 = _CV["fw"] + 24 * 2 + j
                ACT(gc[:, 0:T], bg[0][:], AF.Identity, scale=cv[:, o2:o2 + 1], bias=cvc("fb", j),
                    reads=[bg[1], const_b], writes=[gc_b])
                CP("pool", gh[:, j, :], wk[:, T:T + 2], reads=[wk_b], writes=[gh_b])
                for k in (1, 0):
                    o = _CV["fw"] + 24 * k + j
                    STT("dve", gc[:, 0:T], wk[:, k:k + T], cv[:, o:o + 1], gc[:, 0:T], ALU.mult, ALU.add,
                        reads=[wk_b, gc_b, const_b], writes=[gc_b])
                ACT(gc[:, 0:T], gc[:, 0:T], AF.Gelu_apprx_tanh, reads=[gc_b], writes=[gc_b])
                TT("dve", big_bf[:, T * j:T * (j + 1)], gc[:, 0:T], bv[0][:], ALU.mult,
                   reads=[gc_b, bv[1]], writes=[act_b[j]])
            chk(11)
            for m in range(KD):
                bk = nbank()
                for th3 in range(3):
                    arhs = [big_bf[:, T * (8 * th3 + k):T * (8 * th3 + k + 1)] for k in range(KD)]
                    proj_group(arhs, act_b[8 * th3:8 * th3 + 8], bank=bk, first=(th3 == 0), last=(th3 == 2))
                TT("dve", xT[:, m, :], bk[0][:], xT[:, m, :], ALU.add, reads=[bk[1], xT_b[m]], writes=[xT_b[m]])
                if m >= 2:
                    norm_acc(m - 2, m == 2, False)
            norm_acc(KD - 2, False, False)
            norm_acc(KD - 1, False, True)

            chk(12)
            rs, rs_b = norm_finish()
            for k in range(KD):
                STT("dve", xT[:, k, :], xT[:, k, :], cvc("g_fin", k), rs[:, 0:T], ALU.mult, ALU.mult,
                    reads=[xT_b[k], rs_b, const_b], writes=[xT_b[k]])
            for tt in range(4):
                yo, yo_b = yout[tt % 2], yout_b[tt % 2]
                for half in range(2):
                    bk, bk_b = nbank()
                    for kk in range(4):
                        k = 4 * half + kk
                        TR(bk[:, P * kk:P * (kk + 1)], xT[:, k, P * tt:P * (tt + 1)], reads=[xT_b[k], const_b],
                           writes=[bk_b], sig=(kk == 3))
                    if half == 0:
                        ACT(yo[:, 0:T], bk[:], AF.Identity, reads=[bk_b], writes=[yo_b])
                    else:
                        CP("dve", yo[:, T:2 * T], bk[:], reads=[bk_b], writes=[yo_b])
                pr.dma("act", ("yout", tt % 2), y_d[t0 + P * tt:t0 + P * (tt + 1), :], yo[:], reads=[yo_b])
          except _Stop:
            break

        pr.wait_all("act", [("yout", 0), ("yout", 1)])
        pr.wait_all("sp", [("yout", 0), ("yout", 1)])

        def replay(name, eng):
            for waits, emit, inc, dsem in pr.ops[name]:
                for k, v in waits:
                    eng.wait_ge(sems[k], v)
                if emit is None:
                    continue
                ins = emit(eng)
                if dsem is not None:
                    ins.then_inc(sems[dsem], 16)
                elif inc:
                    ins.then_inc(sems[name], 1)

        with nc.Block() as block:
            @block.sync
            def _(e):
                replay("sp", e)

            @block.tensor
            def _(e):
                replay("pe", e)

            @block.scalar
            def _(e):
                replay("act", e)

            @block.vector
            def _(e):
                replay("dve", e)

            @block.gpsimd
            def _(e):
                replay("pool", e)
    nc._prog = pr
    return nc


def _fm(v):
    v = np.asarray(v, np.float32)
    return np.ascontiguousarray(v.reshape(-1, P).T)


def _slab(W, r0, c0):
    return W[r0:r0 + 1024, c0:c0 + P].reshape(KD, P, P).transpose(1, 0, 2)


def _weight_stream(w_in, w_a, w_b, w_out, w_up, w_down):
    slabs = []
    for i in range(KD + 1):
        if i < KD:
            slabs.append(_slab(w_in, 0, P * i))
            slabs.append(_slab(w_in, 0, 1024 + P * i))
        if i >= 1:
            slabs.append(_slab(w_in, 0, 3072 + P * (i - 1)))
            slabs.append(_slab(w_in, 0, 2048 + P * (i - 1)))
    for m in range(KD):
        slabs.append(_slab(w_in, 0, 4096 + P * m))
        slabs.append(_slab(w_in, 0, 5120 + P * m))
    for m in range(KD):
        slabs.append(_slab(w_a, 0, P * m))
        slabs.append(_slab(w_b, 0, P * m))
    for m in range(KD):
        slabs.append(_slab(w_out, 0, P * m))
    for j in range(KF):
        slabs.append(_slab(w_up, 0, P * j))
        slabs.append(_slab(w_up, 0, DFF + P * j))
    for m in range(KD):
        for th3 in range(3):
            slabs.append(_slab(w_down, 1024 * th3, P * m))
    assert len(slabs) == 4 * NGRP
    st = np.stack(slabs).reshape(NGRP, 4, P, KD * P).transpose(0, 2, 1, 3)
    return np.ascontiguousarray(st.reshape(NGRP, P, 4096), dtype=np.float32)


def _cvec(inp):
    cvv = np.zeros((P, NV), np.float32)

    def put(name, arr):
        arr = np.asarray(arr, np.float32)
        cvv[:, _CV[name]:_CV[name] + arr.shape[1]] = arr
    put("g_mix", _fm(inp["g_mix"][0]))
    put("lcw", np.concatenate([_fm(inp["lru_conv_w"][0, k]) for k in range(4)], axis=1))
    put("lcb", _fm(inp["lru_conv_b"][0]))
    put("lba", _fm(inp["lru_ba"][0]))
    put("lbx", _fm(inp["lru_bx"][0]))
    put("lam", _fm(inp["lru_lambda"][0]))
    put("cw", np.concatenate([_fm(inp["conf_dw_w"][0, k]) for k in range(31)], axis=1))
    put("cb", _fm(inp["conf_dw_b"][0]))
    put("lng", _fm(inp["conf_ln_g"][0]))
    put("lnb", _fm(inp["conf_ln_b"][0]))
    put("bga", _fm(inp["b_gate"][0, :D]))
    put("bgb", _fm(inp["b_gate"][0, D:]))
    put("g_ffn", _fm(inp["g_ffn"][0]))
    put("fw", np.concatenate([_fm(inp["ffn_dw_w"][0, k]) for k in range(3)], axis=1))
    put("fb", _fm(inp["ffn_dw_b"][0]))
    put("g_fin", _fm(inp["g_final"]))
    return cvv


_NC_CACHE = {}


def kernel(**inputs):
    inp = {k: np.asarray(v) for k, v in inputs.items()}
    if "nc" not in _NC_CACHE:
        _NC_CACHE["nc"] = build_program()
    nc = _NC_CACHE["nc"]
    wst = _weight_stream(inp["w_in"][0], inp["w_lru_out"][0], inp["w_conf_out"][0], inp["w_out"][0],
                         inp["w_up"][0], inp["w_down"][0])
    cvv = _cvec(inp)
    wa = np.ascontiguousarray(inp["lru_wa"][0], dtype=np.float32)
    wx = np.ascontiguousarray(inp["lru_wx"][0], dtype=np.float32)
    x = np.asarray(inp["x"], np.float32)
    in_maps = [{"x": np.ascontiguousarray(x[b]), "wst": wst, "cvec": cvv, "lru_wa": wa, "lru_wx": wx}
               for b in range(8)]
    res = run_bass_kernel_spmd(nc, in_maps, core_ids=list(range(8)))
    return np.stack([np.asarray(res.results[b]["y"], dtype=np.float32).reshape(S, D) for b in range(8)])
```
